# Optimizing a Trainium2 kernel written in Bass

```python
import math
import jax, jax.numpy as jnp
from jax import lax
import numpy as np

D_MODEL = 1024
BATCH = 16
SEQ = 256
DEPTH = 4
DEC_BATCH = 2
DEC_SEQ = 1024
PAST_LEN = 256

GRID_W = 64
D_MIX = D_MODEL
D_RNN = D_MIX // 2
N_LRU_HEADS = 8
LRU_HEAD_DIM = D_RNN // N_LRU_HEADS
LRU_C = 8.0
CONV_WIDTH = 4
CONV_LEFT = CONV_WIDTH // 2
D_POOL = D_MIX // 4
POOL_WINDOWS = (2, 4, 8, 16)
N_POOL_GROUPS = len(POOL_WINDOWS)
POOL_GROUP_DIM = D_POOL // N_POOL_GROUPS
D_FOURIER = D_MIX // 4
N_FOURIER_HEADS = 4
FOURIER_HEAD_DIM = D_FOURIER // N_FOURIER_HEADS
D_IN = 2 * D_RNN + D_POOL + D_FOURIER
D_FF = 4 * D_MODEL
N_MOD = 6
EPS = 1e-6

kernel_name = "hybrid_rglru_pool_fourier_diffusion_step"


def rms_norm(x, g):
    xf = x.astype(jnp.float32)
    y = xf * lax.rsqrt(jnp.mean(xf * xf, axis=-1, keepdims=True) + EPS)
    return (y * g.astype(jnp.float32)).astype(x.dtype)


def centred_dwconv(x, w, b):
    L = x.shape[1]
    xp = jnp.pad(x, ((0, 0), (CONV_LEFT, CONV_WIDTH - 1 - CONV_LEFT), (0, 0)))
    y = b
    for k in range(CONV_WIDTH):
        y = y + xp[:, k:k + L] * w[k]
    return y


def _lin_combine(e1, e2):
    a1, b1 = e1
    a2, b2 = e2
    return a1 * a2, a2 * b1 + b2


def rg_lru(x, w_r, b_r, w_i, b_i, lam, h0, reverse):
    B, L, _ = x.shape
    xh = x.reshape(B, L, N_LRU_HEADS, LRU_HEAD_DIM)
    r = jax.nn.sigmoid((jnp.einsum('blhi,hij->blhj', xh, w_r).reshape(B, L, D_RNN) + b_r).astype(jnp.float32))
    i = jax.nn.sigmoid((jnp.einsum('blhi,hij->blhj', xh, w_i).reshape(B, L, D_RNN) + b_i).astype(jnp.float32))
    log_a = -LRU_C * r * jax.nn.softplus(-lam.astype(jnp.float32))
    a = jnp.exp(log_a)
    mult = jnp.sqrt(-jnp.expm1(2.0 * log_a))
    bterm = mult * (i * x.astype(jnp.float32))
    a_cum, b_cum = lax.associative_scan(_lin_combine, (a, bterm), axis=1, reverse=reverse)
    hs = a_cum * h0.astype(jnp.float32)[:, None, :] + b_cum
    final = hs[:, 0] if reverse else hs[:, -1]
    return hs, final


def _bounds(n, w):
    idx = np.arange(n)
    lo = np.clip(idx - w // 2, 0, n)
    hi = np.clip(idx - w // 2 + w, 0, n)
    return lo, hi, (hi - lo).astype(np.float32)


def pool_1d(x, w):
    L = x.shape[1]
    cs = jnp.pad(jnp.cumsum(x, axis=1), ((0, 0), (1, 0), (0, 0)))
    lo, hi, cnt = _bounds(L, w)
    return (cs[:, hi] - cs[:, lo]) / jnp.asarray(cnt)[None, :, None]


def pool_2d(x, w, rows):
    B, L, C = x.shape
    g = x.reshape(B, rows, GRID_W, C)
    cs = jnp.cumsum(jnp.cumsum(g, axis=1), axis=2)
    cs = jnp.pad(cs, ((0, 0), (1, 0), (1, 0), (0, 0)))
    rlo, rhi, rc = _bounds(rows, w)
    clo, chi, cc = _bounds(GRID_W, w)
    top_hi = cs[:, rhi]
    top_lo = cs[:, rlo]
    s = top_hi[:, :, chi] - top_lo[:, :, chi] - top_hi[:, :, clo] + top_lo[:, :, clo]
    cnt = jnp.asarray(np.outer(rc, cc).astype(np.float32))
    return (s / cnt[None, :, :, None]).reshape(B, L, C)


def mixer(h, p, l, h0, grid_rows):
    B, L, _ = h.shape
    u = h @ p['w_in'][l]
    xr = u[..., :D_RNN]
    xg = u[..., D_RNN:2 * D_RNN]
    xp = u[..., 2 * D_RNN:2 * D_RNN + D_POOL]
    xf = u[..., 2 * D_RNN + D_POOL:]

    xc = centred_dwconv(xr, p['conv_w'][l], p['conv_b'][l])
    hf, sf = rg_lru(xc, p['lru_wr'][l, 0], p['lru_br'][l, 0], p['lru_wi'][l, 0], p['lru_bi'][l, 0],
                    p['lru_lambda'][l, 0], h0[:, 0], reverse=False)
    hb, sb = rg_lru(xc, p['lru_wr'][l, 1], p['lru_br'][l, 1], p['lru_wi'][l, 1], p['lru_bi'][l, 1],
                    p['lru_lambda'][l, 1], h0[:, 1], reverse=True)
    out_a = ((hf + hb) * jax.nn.gelu(xg.astype(jnp.float32))).astype(h.dtype)
    final_state = jnp.stack([sf, sb], axis=1)

    xpf = xp.astype(jnp.float32)
    pooled = []
    for gi, w in enumerate(POOL_WINDOWS):
        xs = xpf[..., gi * POOL_GROUP_DIM:(gi + 1) * POOL_GROUP_DIM]
        m = pool_1d(xs, w) if grid_rows is None else pool_2d(xs, w, grid_rows)
        pooled.append(m - xs)
    pooled = jnp.stack(pooled, axis=2).astype(h.dtype)
    out_b = jnp.einsum('blgi,gij->blgj', pooled, p['pool_w'][l]).reshape(B, L, D_POOL) * p['pool_scale'][l]

    xfh = xf.astype(jnp.float32).reshape(B, L, N_FOURIER_HEADS, FOURIER_HEAD_DIM)
    four = jnp.real(jnp.fft.fft2(xfh, axes=(1, 3), norm='ortho')).reshape(B, L, D_FOURIER).astype(h.dtype)
    out_c = four @ p['fourier_w'][l]

    cat = jnp.concatenate([out_a, out_b, out_c], axis=-1)
    return cat @ p['w_out'][l], final_state


def run_layer(x, cond, p, l, h0, grid_rows):
    mods = jax.nn.silu(cond) @ p['w_mod'][l] + p['b_mod'][l]
    mods = mods.reshape(mods.shape[:-1] + (N_MOD, D_MODEL))
    if mods.ndim == 2:
        mods = mods[None]
    mods = mods[:, None]
    sh1, sc1, g1, sh2, sc2, g2 = [mods[:, :, k] for k in range(N_MOD)]
    h = rms_norm(x, p['norm1_g'][l]) * (1.0 + sc1) + sh1
    out, st = mixer(h, p, l, h0, grid_rows)
    x = x + g1 * out
    h2 = rms_norm(x, p['norm2_g'][l]) * (1.0 + sc2) + sh2
    ff = jnp.square(jax.nn.relu(h2 @ p['mlp_w1'][l])) @ p['mlp_w2'][l]
    x = x + g2 * ff
    return x, st


def setup_inputs(seed: int = 0) -> dict:
    key = jax.random.key(seed)
    ks = jax.random.split(key, 24)

    def nrm(k, shape, scale):
        return jax.random.normal(k, shape, jnp.float32) * scale

    u = jax.random.uniform(ks[16], (DEPTH, 2, D_RNN), jnp.float32, minval=0.9, maxval=0.999)
    s = u ** (1.0 / LRU_C)
    lam = jnp.log(s) - jnp.log1p(-s)
    return {
        'x_prompt': nrm(ks[0], (BATCH, SEQ, D_MODEL), 1.0),
        'x_sample': nrm(ks[1], (DEC_BATCH, DEC_SEQ, D_MODEL), 1.0),
        'state_rglru': nrm(ks[2], (DEC_BATCH, DEPTH, 2, D_RNN), 0.5),
        'c': nrm(ks[3], (DEC_BATCH, D_MODEL), 1.0),
        'c_ctx': nrm(ks[4], (D_MODEL,), 1.0),
        'norm1_g': 1.0 + nrm(ks[5], (DEPTH, D_MODEL), 0.1),
        'norm2_g': 1.0 + nrm(ks[6], (DEPTH, D_MODEL), 0.1),
        'final_g': 1.0 + nrm(ks[7], (D_MODEL,), 0.1),
        'w_mod': nrm(ks[8], (DEPTH, D_MODEL, N_MOD * D_MODEL), 0.5 * D_MODEL ** -0.5),
        'b_mod': nrm(ks[9], (DEPTH, N_MOD * D_MODEL), 0.02),
        'w_in': nrm(ks[10], (DEPTH, D_MODEL, D_IN), D_MODEL ** -0.5),
        'conv_w': nrm(ks[11], (DEPTH, CONV_WIDTH, D_RNN), CONV_WIDTH ** -0.5),
        'conv_b': nrm(ks[12], (DEPTH, D_RNN), 0.02),
        'lru_wr': nrm(ks[13], (DEPTH, 2, N_LRU_HEADS, LRU_HEAD_DIM, LRU_HEAD_DIM), LRU_HEAD_DIM ** -0.5),
        'lru_br': nrm(ks[14], (DEPTH, 2, D_RNN), 0.1),
        'lru_wi': nrm(ks[15], (DEPTH, 2, N_LRU_HEADS, LRU_HEAD_DIM, LRU_HEAD_DIM), LRU_HEAD_DIM ** -0.5),
        'lru_bi': nrm(ks[17], (DEPTH, 2, D_RNN), 0.1),
        'lru_lambda': lam,
        'pool_w': nrm(ks[18], (DEPTH, N_POOL_GROUPS, POOL_GROUP_DIM, POOL_GROUP_DIM), POOL_GROUP_DIM ** -0.5),
        'pool_scale': 1.0 + nrm(ks[19], (DEPTH, D_POOL), 0.1),
        'fourier_w': nrm(ks[20], (DEPTH, D_FOURIER, D_FOURIER), D_FOURIER ** -0.5),
        'w_out': nrm(ks[21], (DEPTH, D_MIX, D_MODEL), D_MIX ** -0.5),
        'mlp_w1': nrm(ks[22], (DEPTH, D_MODEL, D_FF), D_MODEL ** -0.5),
        'mlp_w2': nrm(ks[23], (DEPTH, D_FF, D_MODEL), D_FF ** -0.5),
    }


def reference(x_prompt, x_sample, state_rglru, c, c_ctx, norm1_g, norm2_g, final_g, w_mod, b_mod, w_in,
              conv_w, conv_b, lru_wr, lru_br, lru_wi, lru_bi, lru_lambda, pool_w, pool_scale, fourier_w,
              w_out, mlp_w1, mlp_w2):
    p = {'norm1_g': norm1_g, 'norm2_g': norm2_g, 'w_mod': w_mod, 'b_mod': b_mod, 'w_in': w_in,
         'conv_w': conv_w, 'conv_b': conv_b, 'lru_wr': lru_wr, 'lru_br': lru_br, 'lru_wi': lru_wi,
         'lru_bi': lru_bi, 'lru_lambda': lru_lambda, 'pool_w': pool_w, 'pool_scale': pool_scale,
         'fourier_w': fourier_w, 'w_out': w_out, 'mlp_w1': mlp_w1, 'mlp_w2': mlp_w2}

    xc = x_prompt
    h0_ctx = jnp.zeros((x_prompt.shape[0], 2, D_RNN), x_prompt.dtype)
    states = []
    for l in range(DEPTH):
        xc, st = run_layer(xc, c_ctx, p, l, h0_ctx, None)
        states.append(st.astype(x_prompt.dtype))
    y_prompt = rms_norm(xc, final_g)
    new_state_rglru = jnp.stack(states, axis=1)

    rows = x_sample.shape[1] // GRID_W
    xs = x_sample
    for l in range(DEPTH):
        xs, _ = run_layer(xs, c, p, l, state_rglru[:, l], rows)
    y_sample = rms_norm(xs, final_g)

    return (y_prompt, y_sample, new_state_rglru)
```

```python
import numpy as np
import ml_dtypes
from contextlib import ExitStack
import concourse.bass as bass
import concourse.mybir as mybir
from concourse.bass_utils import run_bass_kernel_spmd
from concourse.alu_op_type import AluOpType as ALU

F32 = mybir.dt.float32
BF16 = mybir.dt.bfloat16
AF = mybir.ActivationFunctionType

D = 1024
T = 1024
DEPTH = 4
DRNN = 512
DIN = 1536
DFF = 4096
NSEG = 4
SEGP = 260
EPS = 1e-6
NCORES = 8
NSLOT = 3

PP_FIELDS = [("cond", 8), ("n1g", 32), ("n2g", 32), ("fg", 8), ("bmod", 192), ("convw", 64),
             ("convb", 16), ("br", 32), ("bi", 32), ("lam", 32), ("pscale", 8), ("h0", 32),
             ("flag", 1)]
PP_OFF = {}
_o = 0
for _n, _w in PP_FIELDS:
    PP_OFF[_n] = _o
    _o += _w
NP_ = _o


class Sched:
    ENG = ("pe", "act", "dve", "pool", "sp")

    def __init__(self):
        self.q = {e: [] for e in self.ENG}
        self.cnt = {}
        self.known = {e: {} for e in self.ENG}
        self.tiles = {}
        self.bank = 0

    def _deps(self, eng, r, w):
        need = {}
        for k in r:
            t = self.tiles.get(k)
            if t and t["w"]:
                sk, v = t["w"]
                need[sk] = max(need.get(sk, 0), v)
        for k in w:
            t = self.tiles.get(k)
            if t:
                if t["w"]:
                    sk, v = t["w"]
                    need[sk] = max(need.get(sk, 0), v)
                for sk, v in t["r"].items():
                    need[sk] = max(need.get(sk, 0), v)
        waits = []
        for sk, v in need.items():
            if sk == "pe" and eng == "pe":
                continue
            if self.known[eng].get(sk, 0) >= v:
                continue
            self.known[eng][sk] = v
            waits.append((sk, v))
        return waits

    def _record(self, ev, r, w):
        for k in w:
            self.tiles[k] = {"w": ev, "r": {}}
        for k in r:
            t = self.tiles.setdefault(k, {"w": None, "r": {}})
            t["r"][ev[0]] = max(t["r"].get(ev[0], 0), ev[1])

    def op(self, eng, fn, r=(), w=(), inc=True):
        waits = self._deps(eng, r, w)
        c = self.cnt.get(eng, 0)
        if inc:
            c += 1
            self.cnt[eng] = c
            ev = (eng, c)
        else:
            ev = (eng, c + 1)
        self.q[eng].append((waits, fn, (eng, 1) if inc else None))
        self._record(ev, r, w)

    def dma(self, queue, fn, r=(), w=(), sem="su"):
        waits = self._deps(queue, r, w)
        c = self.cnt.get(sem, 0) + 16
        self.cnt[sem] = c
        self.q[queue].append((waits, fn, (sem, 16)))
        self._record((sem, c), r, w)

    def seal(self, sem):
        tot = self.cnt.get(sem, 0)
        for t in self.tiles.values():
            if t["w"] and t["w"][0] == sem:
                t["w"] = (sem, tot)

    def next_bank(self):
        b = self.bank
        self.bank = (b + 1) % 8
        return b

    def sem_names(self):
        names = set(self.cnt.keys())
        return sorted(names)


def build(nl=DEPTH, debug=False):
    nc = bass.Bass("TRN2", target_bir_lowering=False)
    S = Sched()

    def din(name, shape):
        return nc.dram_tensor(name, list(shape), F32, kind="ExternalInput").ap()

    xT_d = din("xT", [D, T])
    pp_d = din("pp", [128, NP_])
    cs64_d = din("cs64", [256, 512])
    cl_d = nc.dram_tensor("cl", [T, T], BF16, kind="ExternalInput").ap()
    sln_d = nc.dram_tensor("sln", [T, T], BF16, kind="ExternalInput").ap()
    pm_d = nc.dram_tensor("pm", [4, T, T], BF16, kind="ExternalInput").ap()
    wmod_d = din("w_mod", [DEPTH, D, 6 * D])
    win_d = din("w_in", [DEPTH, D, DIN])
    wout_d = din("w_out", [DEPTH, D, D])
    w1_d = din("mlp_w1", [DEPTH, D, DFF])
    w2_d = din("mlp_w2", [DEPTH, DFF, D])
    wr_d = din("lru_wr", [DEPTH, 2, 8, 64, 64])
    wi_d = din("lru_wi", [DEPTH, 2, 8, 64, 64])
    pw_d = din("pool_w", [DEPTH, 4, 64, 64])
    fw_d = din("fourier_w", [DEPTH, 256, 256])
    yT_d = nc.dram_tensor("yT", [D, T], F32, kind="ExternalOutput").ap()
    st_d = nc.dram_tensor("st", [128, 128], F32, kind="ExternalOutput").ap()

    es = ExitStack()
    with es:
        def sb(name, shape, dt):
            return es.enter_context(nc.sbuf_tensor(name, list(shape), dt))

        x_sb = sb("x_sb", [128, 8, T], F32)
        hT = sb("hT", [128, 8, T], BF16)
        ring = [sb(f"ring{i}", [128, 8, 1024], BF16) for i in range(NSLOT)]
        xrp = sb("xrp", [128, 4, NSEG * SEGP], F32)
        mixB = sb("mixB", [128, 8, 1024], BF16)
        gel = sb("gel", [128, 4, T], BF16)
        xc = sb("xc", [128, T], F32)
        xcb = sb("xcb", [128, T], BF16)
        tA = sb("tA", [128, T], F32)
        tB = sb("tB", [128, T], F32)
        tC = sb("tC", [128, T], F32)
        tD = sb("tD", [128, T], F32)
        tE = sb("tE", [128, T], F32)
        tF = sb("tF", [128, T], F32)
        tG = sb("tG", [128, T], F32)
        tH = sb("tH", [128, T], F32)
        tI = sb("tI", [128, T], F32)
        tJ = sb("tJ", [128, T], F32)
        tK = sb("tK", [128, T], F32)
        tL = sb("tL", [128, T], F32)
        pp = sb("pp_sb", [128, NP_], F32)
        mods = sb("mods", [128, DEPTH * 6 * 8], F32)
        gmod = sb("gmod", [128, DEPTH * 2 * 8], F32)
        lrup = sb("lrup", [128, 4 * 32], F32)
        gw = [sb("gw0", [128, 16, 128], BF16)] * 2
        pw = [sb("pw0", [128, 2, 128], BF16)] * 2
        fw = [sb("fw0", [128, 2, 256], BF16)] * 2
        cs64 = sb("cs64_sb", [128, 2, 512], BF16)
        ones = sb("ones", [128, 128], BF16)
        s_sb = sb("s_sb", [128, 8], BF16)
        st_sb = sb("st_sb", [128, 128], F32)
        ps = [es.enter_context(nc.psum_tensor(f"ps{i}", [128, 512], F32)) for i in range(8)]

        hid0 = xrp[:].rearrange("p a b -> p (a b)").bitcast(BF16)[:, 0:8192].rearrange("p (c t) -> p c t", c=8)
        hid1 = mixB
        xp_v = mixB[:, 0:2, :].rearrange("p a (b c) -> p (a b) c", c=256)
        xfT = mixB[:, 2:4, :]
        fourT = mixB[:, 2:4, :]
        poolT = mixB[:, 4:6, :]
        Y_v = mixB[:, 4:8, :].rearrange("p a (b c) -> p (a b) c", c=512)
        xrp_v = xrp[:].rearrange("p j (s t) -> p j s t", t=SEGP)

        def ppc(name, idx=0, w=1):
            o = PP_OFF[name] + idx
            return pp[:, o:o + w]

        def modc(l, k, c):
            o = (l * 6 + k) * 8 + c
            return mods[:, o:o + 1]

        S.dma("sp", lambda e: e.dma_start(out=pp[:], in_=pp_d), w=[("pp",)], sem="su")
        S.dma("sp", lambda e: e.dma_start(out=x_sb[:], in_=xT_d.rearrange("(c p) t -> p c t", p=128)),
              w=[("x", c, h) for c in range(8) for h in range(2)], sem="su")
        S.dma("pool", lambda e: e.dma_start(out=cs64[:], in_=cs64_d.rearrange("(k p) n -> p k n", p=128)),
              w=[("cs64",)], sem="suc")
        S.seal("su")
        S.op("dve", lambda e: e.memset(ones[:], 1.0), w=[("ones",)])
        for i in range(1):
            S.op("dve", lambda e, i=i: e.memset(gw[i][:], 0.0), w=[("gw", i)])
            S.op("dve", lambda e, i=i: e.memset(pw[i][:], 0.0), w=[("pw", i)])
        S.op("dve", lambda e: e.memset(st_sb[:], 0.0), w=[("st",)])
        S.op("dve", lambda e: e.memset(xrp[:], 0.0), w=[("xrp", j) for j in range(4)])
        S.op("act", lambda e: e.activation(out=s_sb[:], in_=ppc("cond", 0, 8), func=AF.Silu),
             r=[("pp",)], w=[("s",)])
        S.op("dve", lambda e: e.tensor_scalar(lrup[:, 0:32], ppc("br", 0, 32), 0.5, None, ALU.mult),
             r=[("pp",)], w=[("lrup", 0)])
        S.op("dve", lambda e: e.tensor_scalar(lrup[:, 32:64], ppc("bi", 0, 32), 0.5, None, ALU.mult),
             r=[("pp",)], w=[("lrup", 1)])
        S.op("act", lambda e: e.activation(out=lrup[:, 96:128], in_=ppc("lam", 0, 32), func=AF.Exp, scale=-1.0),
             r=[("pp",)], w=[("lrup", 3)])
        S.op("act", lambda e: e.activation(out=lrup[:, 96:128], in_=lrup[:, 96:128], func=AF.Ln, bias=1.0),
             r=[("lrup", 3)], w=[("lrup", 3)])
        S.op("dve", lambda e: e.tensor_scalar(lrup[:, 64:96], lrup[:, 96:128], -4.0, None, ALU.mult),
             r=[("lrup", 3)], w=[("lrup", 2)])

        def hbr(l, d, j):
            o = (l * 2 + d) * 4 + j
            return lrup[:, o:o + 1]

        def hbi(l, d, j):
            o = 32 + (l * 2 + d) * 4 + j
            return lrup[:, o:o + 1]

        def kap2(l, d, j):
            o = 64 + (l * 2 + d) * 4 + j
            return lrup[:, o:o + 1]

        items = []

        def kview(ap2d):
            return ap2d.rearrange("(kc p) n -> p kc n", p=128)

        def mod_item(l, k):
            return (kview(wmod_d[l])[:, :, k * D:(k + 1) * D], 1024, ("mod", l, k))
        items.append(mod_item(0, 0))
        items.append(mod_item(0, 1))
        for l in range(nl):
            if l > 0:
                items.append(mod_item(l, 4))
                items.append(mod_item(l, 5))
            items.append((kview(win_d[l])[:, :, 0:768], 768, ("winA", l)))
            if l == 0:
                items.append(mod_item(0, 2))
            items.append((kview(win_d[l])[:, :, 768:1536], 768, ("winB", l)))
            if l == 0:
                items.append(mod_item(0, 3))
            items.append((kview(cl_d), 1024, ("cl", l)))
            items.append((kview(sln_d), 1024, ("sln", l)))
            for g in range(4):
                items.append((kview(pm_d[g]), 1024, ("pm", l, g)))
                if l == 0 and g < 2:
                    items.append(mod_item(0, 4 + g))
            if l + 1 < nl:
                items.append(mod_item(l + 1, 0))
                items.append(mod_item(l + 1, 1))
            items.append((kview(wout_d[l]), 1024, ("wout", l)))
            for tag, q in (("w1", 0), ("w1", 1), ("w2", 0), ("w1", 2), ("w2", 1), ("w1", 3), ("w2", 2), ("w2", 3)):
                if tag == "w1":
                    items.append((kview(w1_d[l])[:, :, q * 1024:(q + 1) * 1024], 1024, ("w1", l, q)))
                else:
                    items.append((kview(w2_d[l][q * 1024:(q + 1) * 1024, :]), 1024, ("w2", l, q)))
                if l + 1 < nl and tag == "w2" and q in (0, 1):
                    items.append(mod_item(l + 1, 2 + q))
        item_idx = {it[2]: i for i, it in enumerate(items)}
        loaded = [0]

        def advance(i):
            while loaded[0] < min(len(items), i + NSLOT):
                j = loaded[0]
                src, n, _ = items[j]
                slot = j % NSLOT
                if items[j][2][0] in ("cl", "sln", "pm"):
                    S.dma("sp", lambda e, slot=slot, src=src, n=n: e.dma_start(out=ring[slot][:, :, 0:n], in_=src),
                          w=[("ring", slot)], sem=f"ringS{slot}")
                else:
                    S.dma("pool", lambda e, slot=slot, src=src, n=n: e.dma_start(out=ring[slot][:, :, 0:n], in_=src),
                          w=[("ring", slot)], sem=f"ring{slot}")
                loaded[0] += 1

        def slot_of(tag):
            i = item_idx[tag]
            assert i < loaded[0], tag
            return i % NSLOT

        def mm_group(out_ap, pairs, reads, bank):
            n = len(pairs)

            def fn(e):
                ins = None
                for i, (l_, r_) in enumerate(pairs):
                    ins = e.matmul(out_ap, l_, r_, start=(i == 0), stop=(i == n - 1))
                return ins
            S.op("pe", fn, r=reads, w=[("ps", bank)])

        def small_weights(l):
            i = 0
            for d in range(2):
                for g, src in ((0, wr_d), (1, wi_d)):
                    for e_ in range(2):
                        base = (d * 2 + g) * 4
                        dst = gw[i][64 * e_:64 * e_ + 64, base:base + 4, 64 * e_:64 * e_ + 64]
                        s_ap = src[l, d].rearrange("(j e) r c -> e r j c", e=2)[e_]
                        S.dma("pool", lambda e, dst=dst, s_ap=s_ap: e.dma_start(out=dst, in_=s_ap),
                              w=[("gw", i)], sem=f"sw{i}")
            for e_ in range(2):
                dst = pw[i][64 * e_:64 * e_ + 64, 0:2, 64 * e_:64 * e_ + 64]
                s_ap = pw_d[l].rearrange("(pr e) r c -> e r pr c", e=2)[e_]
                S.dma("pool", lambda e, dst=dst, s_ap=s_ap: e.dma_start(out=dst, in_=s_ap),
                      w=[("pw", i)], sem=f"sw{i}")
            S.dma("pool", lambda e: e.dma_start(out=fw[i][:], in_=fw_d[l].rearrange("(k p) n -> p k n", p=128)),
                  w=[("fw", i)], sem=f"sw{i}")
            S.seal(f"sw{i}")

        def compute_mods(l, ks):
            for k in ks:
                advance(item_idx[("mod", l, k)])
                sl = slot_of(("mod", l, k))
                bank = S.next_bank()

                def fn(e, sl=sl, bank=bank):
                    ins = None
                    for c in range(8):
                        for kc in range(8):
                            ins = e.matmul(ps[bank][:, c:c + 1], ring[sl][:, kc, c * 128:(c + 1) * 128],
                                           s_sb[:, kc:kc + 1], start=(kc == 0), stop=(kc == 7))
                    return ins
                S.op("pe", fn, r=[("ring", sl), ("s",)], w=[("ps", bank)])
                o = (l * 6 + k) * 8
                S.op("dve", lambda e, bank=bank, o=o: e.tensor_tensor(
                    mods[:, o:o + 8], ps[bank][:, 0:8], ppc("bmod", o, 8), ALU.add),
                    r=[("ps", bank), ("pp",)], w=[("mods", l, k)])
            for which, kk, gname in ((0, 1, "n1g"), (1, 4, "n2g")):
                if kk not in ks:
                    continue
                o = (l * 6 + kk) * 8
                go = (l * 2 + which) * 8
                S.op("dve", lambda e, o=o, go=go, gname=gname, l=l: e.scalar_tensor_tensor(
                    gmod[:, go:go + 8], mods[:, o:o + 8], 1.0, ppc(gname, l * 8, 8), ALU.add, ALU.mult),
                    r=[("mods", l, kk), ("pp",)], w=[("gmod", l, which)])

        def sq_buf(which, c):
            if which == 1:
                return mixB[:, c, :], [("mixB", c)]
            return hT[:, c, :], [("hT", c, 0), ("hT", c, 1)]

        def sq_key_h(which, c, h):
            return ("mixB", c) if which == 1 else ("hT", c, h)

        def norm_square(which, c, h=None):
            ap, keys = sq_buf(which, c)
            if h is None:
                S.op("act", lambda e, c=c, ap=ap: e.activation(out=ap, in_=x_sb[:, c, :], func=AF.Square),
                     r=[("x", c, 0), ("x", c, 1)], w=keys)
            else:
                hs_ = slice(h * 512, (h + 1) * 512)
                S.op("act", lambda e, c=c, ap=ap, hs_=hs_: e.activation(out=ap[:, hs_], in_=x_sb[:, c, hs_], func=AF.Square),
                     r=[("x", c, h)], w=[sq_key_h(which, c, h)])

        def norm_half(l, which, h):
            hs_ = slice(h * 512, (h + 1) * 512)
            bank = S.next_bank()
            mm_group(ps[bank][:, :], [(ones[:, :], sq_buf(which, c)[0][:, hs_]) for c in range(8)],
                     [("ones",)] + [sq_key_h(which, c, h) for c in range(8)], bank)
            S.op("act", lambda e, bank=bank: e.activation(
                out=tA[:, hs_], in_=ps[bank][:, :], func=AF.Ln, scale=1.0 / D, bias=EPS),
                r=[("ps", bank)], w=[("tA", h)])
            S.op("act", lambda e: e.activation(out=tA[:, hs_], in_=tA[:, hs_], func=AF.Exp, scale=-0.5),
                 r=[("tA", h)], w=[("tA", h)])
            for c in range(8):
                tmp, tk = (tB, "tB") if c % 2 == 0 else (tC, "tC")
                S.op("dve", lambda e, c=c, tmp=tmp: e.tensor_tensor(
                    tmp[:, hs_], x_sb[:, c, hs_], tA[:, hs_], ALU.mult),
                    r=[("x", c, h), ("tA", h)], w=[(tk, h)])
                go = (l * 2 + which) * 8 + c
                sh = modc(l, 0 if which == 0 else 3, c)
                S.op("act", lambda e, c=c, tmp=tmp, go=go, sh=sh: e.activation(
                    out=hT[:, c, hs_], in_=tmp[:, hs_], func=AF.Identity, scale=gmod[:, go:go + 1], bias=sh),
                    r=[(tk, h), ("gmod", l, which), ("mods", l, 0 if which == 0 else 3)], w=[("hT", c, h)])

        def norm_finish(l, which, mid=None):
            for h in range(2):
                bank = S.next_bank()
                mm_group(ps[bank][:, :], [(ones[:, :], sq_buf(which, c)[0][:, h * 512:(h + 1) * 512]) for c in range(8)],
                         [("ones",)] + [sq_key_h(which, c, h) for c in range(8)], bank)
                S.op("act", lambda e, bank=bank, h=h: e.activation(
                    out=tA[:, h * 512:(h + 1) * 512], in_=ps[bank][:, :], func=AF.Ln, scale=1.0 / D, bias=EPS),
                    r=[("ps", bank)], w=[("tA", h)])
            S.op("act", lambda e: e.activation(out=tA[:, :], in_=tA[:, :], func=AF.Exp, scale=-0.5),
                 r=[("tA", 0), ("tA", 1)], w=[("tA", 0), ("tA", 1)])
            if mid is not None:
                mid()
            if which < 2:
                for h in range(2):
                    hs_ = slice(h * 512, (h + 1) * 512)
                    for c in range(8):
                        tmp, tk = (tB, "tB") if c % 2 == 0 else (tC, "tC")
                        S.op("dve", lambda e, c=c, tmp=tmp, hs_=hs_: e.tensor_tensor(
                            tmp[:, hs_], x_sb[:, c, hs_], tA[:, hs_], ALU.mult),
                            r=[("x", c, h), ("tA", h)], w=[(tk, h)])
                        go = (l * 2 + which) * 8 + c
                        sh = modc(l, 0 if which == 0 else 3, c)
                        S.op("act", lambda e, c=c, tmp=tmp, go=go, sh=sh, hs_=hs_: e.activation(
                            out=hT[:, c, hs_], in_=tmp[:, hs_], func=AF.Identity, scale=gmod[:, go:go + 1], bias=sh),
                            r=[(tk, h), ("gmod", l, which), ("mods", l, 0 if which == 0 else 3)], w=[("hT", c, h)])
            else:
                ftmps = ((tB, "tB"), (tC, "tC"), (tD, "tD"), (tE, "tE"), (tF, "tF"), (tG, "tG"), (tH, "tH"), (tI, "tI"))
                for c in range(8):
                    tmp, tk = ftmps[c]
                    S.op("dve", lambda e, c=c, tmp=tmp: e.tensor_tensor(tmp[:, :], x_sb[:, c, :], tA[:, :], ALU.mult),
                         r=[("x", c, 0), ("x", c, 1), ("tA", 0), ("tA", 1)], w=[(tk, 0), (tk, 1)])
                    S.op("act", lambda e, c=c, tmp=tmp: e.activation(
                        out=tmp[:, :], in_=tmp[:, :], func=AF.Copy, scale=ppc("fg", c)),
                        r=[(tk, 0), (tk, 1), ("pp",)], w=[(tk, 0), (tk, 1)])
                    S.dma("sp", lambda e, c=c, tmp=tmp: e.dma_start(out=yT_d[c * 128:(c + 1) * 128, :], in_=tmp[:, :]),
                          r=[(tk, 0), (tk, 1)], sem="out" + tk)

        def evac_copy(eng, out_ap, bank, wkeys, in_ap=None):
            src = ps[bank][:, :] if in_ap is None else in_ap
            if eng == "act":
                S.op("act", lambda e: e.activation(out=out_ap, in_=src, func=AF.Copy), r=[("ps", bank)], w=wkeys)
            else:
                S.op("dve", lambda e: e.tensor_copy(out_ap, src), r=[("ps", bank)], w=wkeys)

        def hT_half(h):
            return [("hT", c, h) for c in range(8)]

        def lru_pre(l, j):
            S.op("dve", lambda e: e.tensor_scalar(xrp_v[:, j, 1:4, 0:2], xrp_v[:, j, 0:3, 256:258],
                                                   ppc("flag"), None, ALU.mult),
                 r=[("xrp", j), ("pp",)], w=[("xrp", j)])
            S.op("dve", lambda e: e.tensor_scalar(xrp_v[:, j, 0:3, 258:259], xrp_v[:, j, 1:4, 2:3],
                                                   ppc("flag"), None, ALU.mult),
                 r=[("xrp", j), ("pp",)], w=[("xrp", j)])
            xc_v = xc[:, :].rearrange("p (s t) -> p s t", t=256)

            def cw(k):
                o = (l * 4 + k) * 4 + j
                return ppc("convw", o)
            S.op("dve", lambda e: e.tensor_scalar(xc_v, xrp_v[:, j, :, 0:256], cw(0), ppc("convb", l * 4 + j),
                                                   ALU.mult, ALU.add),
                 r=[("xrp", j), ("pp",)], w=[("xc",)])
            for k in range(1, 4):
                S.op("dve", lambda e, k=k: e.scalar_tensor_tensor(xc_v, xrp_v[:, j, :, k:k + 256], cw(k), xc_v,
                                                                  ALU.mult, ALU.add),
                     r=[("xrp", j), ("xc",), ("pp",)], w=[("xc",)])
            S.op("act", lambda e: e.activation(out=xcb[:, :], in_=xc[:, :], func=AF.Copy), r=[("xc",)], w=[("xcb",)])

        TSS = {
            (0, 0): (tA, tB, tC, tD, ("tA", "tB", "tC", "tD")), (0, 1): (tI, tJ, tC, tD, ("tI", "tJ", "tC", "tD")),
            (1, 0): (tE, tF, tG, tH, ("tE", "tF", "tG", "tH")), (1, 1): (tK, tL, tG, tH, ("tK", "tL", "tG", "tH")),
        }

        def k2(name):
            return [(name, 0), (name, 1)]

        def lru_chunk(l, j, defer_post=False):
            i = 0
            TS = (TSS[(0, j % 2)], TSS[(1, j % 2)])
            for d in range(2):
                A_, B_, C_, D_, nm = TS[d]
                for g, dst, dk, hb in ((0, A_, nm[0], hbr(l, d, j)), (1, B_, nm[1], hbi(l, d, j))):
                    for h in range(2):
                        bank = S.next_bank()
                        mm_group(ps[bank][:, :], [(gw[i][:, (d * 2 + g) * 4 + j, :], xcb[:, h * 512:(h + 1) * 512])],
                                 [("gw", i), ("xcb",)], bank)
                        S.op("act", lambda e, bank=bank, dst=dst, h=h, hb=hb: e.activation(
                            out=dst[:, h * 512:(h + 1) * 512], in_=ps[bank][:, :], func=AF.Tanh, scale=0.5, bias=hb),
                            r=[("ps", bank), ("lrup", g)], w=[(dk, h)])
            for d in range(2):
                A_, B_, C_, D_, nm = TS[d]
                kk = kap2(l, d, j)
                S.op("act", lambda e, A_=A_, kk=kk: e.activation(out=A_[:, :], in_=A_[:, :], func=AF.Exp, scale=kk, bias=kk),
                     r=k2(nm[0]) + [("lrup", 2)], w=k2(nm[0]))
            for d in range(2):
                A_, B_, C_, D_, nm = TS[d]
                S.op("act", lambda e, A_=A_, C_=C_: e.activation(out=C_[:, :], in_=A_[:, :], func=AF.Square),
                     r=k2(nm[0]), w=k2(nm[2]))
            for d in range(2):
                A_, B_, C_, D_, nm = TS[d]
                S.op("act", lambda e, C_=C_: e.activation(out=C_[:, :], in_=C_[:, :], func=AF.Sqrt, scale=-1.0, bias=1.0),
                     r=k2(nm[2]), w=k2(nm[2]))
            for d in range(2):
                A_, B_, C_, D_, nm = TS[d]
                S.op("dve", lambda e, B_=B_: e.scalar_tensor_tensor(B_[:, :], B_[:, :], 1.0, xc[:, :], ALU.add, ALU.mult),
                     r=k2(nm[1]) + [("xc",)], w=k2(nm[1]))
            if j + 1 < 4:
                lru_pre(l, j + 1)
            for d in range(2):
                A_, B_, C_, D_, nm = TS[d]
                S.op("dve", lambda e, B_=B_, C_=C_: e.scalar_tensor_tensor(B_[:, :], C_[:, :], 0.5, B_[:, :], ALU.mult, ALU.mult),
                     r=k2(nm[1]) + k2(nm[2]), w=k2(nm[1]))
                if d == 0:
                    bnd = A_[:, 256:1024].rearrange("p (s t) -> p s t", t=256)[:, :, 0:1]
                else:
                    bnd = A_[:, 0:768].rearrange("p (s t) -> p s t", t=256)[:, :, 255:256]
                S.op("dve", lambda e, bnd=bnd: e.tensor_scalar(bnd, bnd, ppc("flag"), None, ALU.mult),
                     r=k2(nm[0]) + [("pp",)], w=k2(nm[0]))
                h0 = ppc("h0", (l * 2 + d) * 4 + j)
                so = ((l * 2 + d) * 4 + j) * 4
                if d == 0:
                    S.op("dve", lambda e, A_=A_, B_=B_, D_=D_, h0=h0: e.tensor_tensor_scan(
                        D_[:, :], A_[:, :], B_[:, :], h0, ALU.mult, ALU.add),
                        r=k2(nm[0]) + k2(nm[1]) + [("pp",)], w=k2(nm[3]))
                    fin = D_[:, :].rearrange("p (s t) -> p s t", t=256)[:, :, 255:256]
                else:
                    def rv(t_):
                        return bass.AP(t_, T - 1, [[T, 128], [-1, T]])
                    S.op("dve", lambda e, A_=A_, B_=B_, D_=D_, h0=h0, rv=rv: e.tensor_tensor_scan(
                        rv(D_), rv(A_), rv(B_), h0, ALU.mult, ALU.add),
                        r=k2(nm[0]) + k2(nm[1]) + [("pp",)], w=k2(nm[3]))
                    fin = D_[:, :].rearrange("p (s t) -> p s t", t=256)[:, :, 0:1]
                S.op("dve", lambda e, so=so, fin=fin: e.tensor_copy(
                    st_sb[:, so:so + 4].rearrange("p (s o) -> p s o", o=1), fin),
                    r=k2(nm[3]), w=[("st",)])
            def post():
                S.op("pool", lambda e: e.tensor_tensor(tD[:, :], tD[:, :], tH[:, :], ALU.add),
                     r=k2("tD") + k2("tH"), w=k2("tD"))
                S.op("pool", lambda e: e.tensor_tensor(hT[:, j, :], tD[:, :], gel[:, j, :], ALU.mult),
                     r=k2("tD") + [("gel", j)], w=[("hT", j, 0), ("hT", j, 1)])
            if defer_post:
                return post
            post()

        def layer(l):
            i = 0
            if l == 0:
                small_weights(0)
                for c in range(8):
                    norm_square(0, c)
            if l == 0:
                norm_finish(l, 0)
            else:
                compute_mods(l, [4, 5])
            advance(item_idx[("winA", l)])
            sA = slot_of(("winA", l))
            for h in range(2):
                for m in range(6):
                    bank = S.next_bank()
                    mm_group(ps[bank][:, :],
                             [(ring[sA][:, kc, m * 128:(m + 1) * 128], hT[:, kc, h * 512:(h + 1) * 512]) for kc in range(8)],
                             [("ring", sA)] + hT_half(h), bank)
                    if m < 4:
                        S.op("act", lambda e, bank=bank, m=m, h=h: e.activation(
                            out=xrp_v[:, m, 2 * h:2 * h + 2, 2:258],
                            in_=ps[bank][:, :].rearrange("p (s t) -> p s t", t=256), func=AF.Copy),
                            r=[("ps", bank)], w=[("xrp", m)])
                    else:
                        S.op("act", lambda e, bank=bank, m=m, h=h: e.activation(
                            out=gel[:, m - 4, h * 512:(h + 1) * 512], in_=ps[bank][:, :], func=AF.Gelu),
                            r=[("ps", bank)], w=[("gel", m - 4)])
                    if m == 0 and h == 1:
                        lru_pre(l, 0)
                    if m == 3 and h == 1:
                        wo_post0 = lru_chunk(l, 0, defer_post=True)
            if l == 0:
                compute_mods(0, [2])
            advance(item_idx[("winB", l)])
            sB = slot_of(("winB", l))
            def winB_cols(ms):
                for m in ms:
                    for h in range(2):
                        bank = S.next_bank()
                        mm_group(ps[bank][:, :],
                                 [(ring[sB][:, kc, m * 128:(m + 1) * 128], hT[:, kc, h * 512:(h + 1) * 512]) for kc in range(8)],
                                 [("ring", sB)] + hT_half(h), bank)
                        if m < 2:
                            S.op("act", lambda e, bank=bank, m=m, h=h: e.activation(
                                out=gel[:, m + 2, h * 512:(h + 1) * 512], in_=ps[bank][:, :], func=AF.Gelu),
                                r=[("ps", bank)], w=[("gel", m + 2)])
                        else:
                            evac_copy("act", xfT[:, m - 4, h * 512:(h + 1) * 512], bank, [("mixB", 2 + m - 4)])
            winB_cols((0, 1))
            post0 = wo_post0
            winB_cols((4, 5))
            for tc in range(8):
                bank = S.next_bank()
                mm_group(ps[bank][:, 0:256],
                         [(hT[:, kc, tc * 128:(tc + 1) * 128], ring[sB][:, kc, 256:512]) for kc in range(8)],
                         [("ring", sB)] + hT_half(tc // 4), bank)
                evac_copy("act", xp_v[:, tc, :], bank, [("mixB", tc // 4)], in_ap=ps[bank][:, 0:256])
            post0()
            if l == 0:
                compute_mods(0, [3])
            advance(item_idx[("cl", l)])
            sC, sS = slot_of(("cl", l)), slot_of(("sln", l))
            for tc in range(8):
                bank = S.next_bank()
                mm_group(ps[bank][:, :], [(xfT[:, kc, tc * 128:(tc + 1) * 128], cs64[:, kc, :]) for kc in range(2)],
                         [("mixB", 2), ("mixB", 3), ("cs64",)], bank)
                evac_copy("act", Y_v[:, tc, :], bank, [("mixB", 4 + tc // 2)])

            def dft(jj):
                for h in range(2):
                    bank = S.next_bank()
                    pairs = []
                    for tc in range(8):
                        pairs.append((Y_v[:, tc, jj * 128:(jj + 1) * 128], ring[sC][:, tc, h * 512:(h + 1) * 512]))
                        pairs.append((Y_v[:, tc, 256 + jj * 128:256 + (jj + 1) * 128], ring[sS][:, tc, h * 512:(h + 1) * 512]))
                    mm_group(ps[bank][:, :], pairs, [("ring", sC), ("ring", sS)] + [("mixB", 4 + q) for q in range(4)], bank)
                    evac_copy("act", fourT[:, jj, h * 512:(h + 1) * 512], bank, [("mixB", 2 + jj)])
            lru_chunk(l, 1)
            dft(0)
            dft(1)
            for j2 in range(2):
                for h in range(2):
                    bank = S.next_bank()
                    mm_group(ps[bank][:, :],
                             [(fw[i][:, kc, j2 * 128:(j2 + 1) * 128], fourT[:, kc, h * 512:(h + 1) * 512]) for kc in range(2)],
                             [("fw", i), ("mixB", 2), ("mixB", 3)], bank)
                    evac_copy("act", hT[:, 6 + j2, h * 512:(h + 1) * 512], bank, [("hT", 6 + j2, h)])

            def pool_group(g):
                advance(item_idx[("pm", l, g)])
                sP = slot_of(("pm", l, g))
                pr, gi = g // 2, g % 2
                for h in range(2):
                    bank = S.next_bank()
                    mm_group(ps[bank][:, :],
                             [(xp_v[:, tc, pr * 128:(pr + 1) * 128], ring[sP][:, tc, h * 512:(h + 1) * 512]) for tc in range(8)],
                             [("ring", sP), ("mixB", 0), ("mixB", 1)], bank)
                    lo = gi * 64
                    S.op("act", lambda e, bank=bank, lo=lo, pr=pr, h=h: e.activation(
                        out=poolT[lo:lo + 64, pr, h * 512:(h + 1) * 512], in_=ps[bank][lo:lo + 64, :], func=AF.Copy),
                        r=[("ps", bank)], w=[("mixB", 4 + pr)])
            lru_chunk(l, 2)
            pool_group(0)
            if l == 0:
                compute_mods(0, [4])
            pool_group(1)
            if l == 0:
                compute_mods(0, [5])
            pool_group(2)
            pool_group(3)
            for pr in range(2):
                for h in range(2):
                    bank = S.next_bank()
                    mm_group(ps[bank][:, :], [(pw[i][:, pr, :], poolT[:, pr, h * 512:(h + 1) * 512])],
                             [("pw", i), ("mixB", 4 + pr)], bank)
                    S.op("act", lambda e, bank=bank, pr=pr, h=h: e.activation(
                        out=hT[:, 4 + pr, h * 512:(h + 1) * 512], in_=ps[bank][:, :], func=AF.Copy,
                        scale=ppc("pscale", l * 2 + pr)),
                        r=[("ps", bank), ("pp",)], w=[("hT", 4 + pr, h)])
            lru_chunk(l, 3)
            if l + 1 < nl:
                compute_mods(l + 1, [0, 1])
            advance(item_idx[("wout", l)])
            sO = slot_of(("wout", l))
            for h in range(2):
                for m in range(8):
                    bank = S.next_bank()
                    mm_group(ps[bank][:, :],
                             [(ring[sO][:, kc, m * 128:(m + 1) * 128], hT[:, kc, h * 512:(h + 1) * 512]) for kc in range(8)],
                             [("ring", sO)] + hT_half(h), bank)
                    xs = x_sb[:, m, h * 512:(h + 1) * 512]
                    S.op("dve", lambda e, bank=bank, xs=xs, m=m: e.scalar_tensor_tensor(
                        xs, ps[bank][:, :], modc(l, 2, m), xs, ALU.mult, ALU.add),
                        r=[("ps", bank), ("x", m, h), ("mods", l, 2)], w=[("x", m, h)])
                    norm_square(1, m, h)
                norm_half(l, 1, h)
            hid = (hid0, hid1)
            hkey = ("xrp", "mixB")

            def w1_phase(q):
                advance(item_idx[("w1", l, q)])
                s1 = slot_of(("w1", l, q))
                hb_, hk = hid[q % 2], hkey[q % 2]
                for h in range(2):
                    for mm_ in range(8):
                        bank = S.next_bank()
                        mm_group(ps[bank][:, :],
                                 [(ring[s1][:, kc, mm_ * 128:(mm_ + 1) * 128], hT[:, kc, h * 512:(h + 1) * 512]) for kc in range(8)],
                                 [("ring", s1)] + hT_half(h), bank)
                        tmp, tk = ((tC, "tC") if (mm_ % 2 == 0) else (tD, "tD"))
                        S.op("act", lambda e, bank=bank, tmp=tmp, h=h: e.activation(
                            out=tmp[:, h * 512:(h + 1) * 512], in_=ps[bank][:, :], func=AF.Relu),
                            r=[("ps", bank)], w=[(tk, h)])
                        wk = [("xrp", j) for j in range(4)] if hk == "xrp" else [("mixB", mm_)]
                        S.op("dve", lambda e, bank=bank, tmp=tmp, h=h, hb_=hb_, mm_=mm_: e.tensor_tensor(
                            hb_[:, mm_, h * 512:(h + 1) * 512], ps[bank][:, :], tmp[:, h * 512:(h + 1) * 512], ALU.mult),
                            r=[("ps", bank), (tk, h)], w=wk)

            def w2_phase(q, last=False):
                advance(item_idx[("w2", l, q)])
                s2 = slot_of(("w2", l, q))
                hb_, hk = hid[q % 2], hkey[q % 2]
                rk = [("xrp", j) for j in range(4)] if hk == "xrp" else [("mixB", c) for c in range(8)]
                order = [(m, h) for h in range(2) for m in range(8)] if last else [(m, h) for m in range(8) for h in range(2)]
                for (m, h) in order:
                    bank = S.next_bank()
                    mm_group(ps[bank][:, :],
                             [(ring[s2][:, kc, m * 128:(m + 1) * 128], hb_[:, kc, h * 512:(h + 1) * 512]) for kc in range(8)],
                             [("ring", s2)] + rk, bank)
                    xs = x_sb[:, m, h * 512:(h + 1) * 512]
                    S.op("dve", lambda e, bank=bank, xs=xs, m=m: e.scalar_tensor_tensor(
                        xs, ps[bank][:, :], modc(l, 5, m), xs, ALU.mult, ALU.add),
                        r=[("ps", bank), ("x", m, h), ("mods", l, 5)], w=[("x", m, h)])
                    if last:
                        if l + 1 < nl:
                            norm_square(0, m, h)
                            if m == 7:
                                norm_half(l + 1, 0, h)
                        elif h == 1:
                            norm_square(2, m)
            w1_phase(0)
            if l + 1 < nl:
                small_weights(l + 1)
            w1_phase(1)
            w2_phase(0)
            if l + 1 < nl:
                compute_mods(l + 1, [2])
            w1_phase(2)
            w2_phase(1)
            if l + 1 < nl:
                compute_mods(l + 1, [3])
            w1_phase(3)
            w2_phase(2)
            w2_phase(3, last=True)
            if l + 1 < nl:
                S.op("dve", lambda e: e.memset(xrp[:], 0.0), r=[], w=[("xrp", j) for j in range(4)])

        advance(0)
        compute_mods(0, [0, 1])
        for l in range(nl):
            layer(l)
        norm_finish(nl - 1, 2)
        S.dma("sp", lambda e: e.dma_start(out=st_d, in_=st_sb[:]), r=[("st",)], sem="outS")
        out_sems = [n for n in S.cnt if n.startswith("out")]

        sems = {n: es.enter_context(nc.semaphore(f"s_{n}")) for n in S.sem_names()}
        block = es.enter_context(nc.Block())

        def emit(eng_name):
            def body(e):
                for waits, fn, inc in S.q[eng_name]:
                    for sk, v in waits:
                        e.wait_ge(sems[sk], v)
                    ins = fn(e)
                    if inc is not None:
                        ins.then_inc(sems[inc[0]], inc[1])
                if eng_name == "sp":
                    for n in out_sems:
                        e.wait_ge(sems[n], S.cnt[n])
            return body
        block.tensor(emit("pe"))
        block.scalar(emit("act"))
        block.vector(emit("dve"))
        block.gpsimd(emit("pool"))
        block.sync(emit("sp"))
    return nc


def _bounds(n, w):
    idx = np.arange(n)
    lo = np.clip(idx - w // 2, 0, n)
    hi = np.clip(idx - w // 2 + w, 0, n)
    return lo, hi


def _pool1d_mat(n, w):
    lo, hi = _bounds(n, w)
    P = np.zeros((n, n), np.float64)
    for t in range(n):
        P[t, lo[t]:hi[t]] = 1.0 / (hi[t] - lo[t])
    return P


def _const_tables(kind):
    windows = (2, 4, 8, 16)
    if kind == "sample":
        L = 1024
        t = np.arange(L)
        ang = 2.0 * np.pi * ((t[:, None] * t[None, :]) % L) / L
        sc = 1.0 / np.sqrt(L * 64.0)
        cl = np.cos(ang) * sc
        sln = -np.sin(ang) * sc
        pm = []
        for w in windows:
            P = np.kron(_pool1d_mat(16, w), _pool1d_mat(64, w))
            pm.append((P - np.eye(L)).T)
    else:
        L = 256
        t = np.arange(L)
        ang = 2.0 * np.pi * ((t[:, None] * t[None, :]) % L) / L
        sc = 1.0 / np.sqrt(L * 64.0)
        eye4 = np.eye(4)
        cl = np.kron(eye4, np.cos(ang) * sc)
        sln = np.kron(eye4, -np.sin(ang) * sc)
        pm = []
        for w in windows:
            P = np.kron(eye4, _pool1d_mat(L, w))
            pm.append((P - np.eye(4 * L)).T)
    k = np.arange(64)
    a64 = 2.0 * np.pi * ((k[:, None] * k[None, :]) % 64) / 64.0
    c64 = np.kron(np.eye(4), np.cos(a64))
    s64 = np.kron(np.eye(4), np.sin(a64))
    cs64 = np.concatenate([c64, s64], axis=1)
    bf = ml_dtypes.bfloat16
    return (np.ascontiguousarray(cl.astype(np.float32).astype(bf)), np.ascontiguousarray(sln.astype(np.float32).astype(bf)),
            np.ascontiguousarray(np.stack(pm).astype(np.float32).astype(bf)), np.ascontiguousarray(cs64, np.float32))


def _pack_pp(cond, h0, flag, p):
    pp = np.zeros((128, NP_), np.float32)

    def put(name, arr):
        a = np.asarray(arr, np.float32)
        lead = a.shape[:-1]
        c = a.shape[-1] // 128
        a = a.reshape(lead + (c, 128))
        a = np.moveaxis(a, -1, 0).reshape(128, -1)
        o = PP_OFF[name]
        pp[:, o:o + a.shape[1]] = a
    put("cond", cond)
    put("n1g", p["norm1_g"])
    put("n2g", p["norm2_g"])
    put("fg", p["final_g"])
    put("bmod", np.asarray(p["b_mod"]).reshape(DEPTH, 6, D))
    put("convw", p["conv_w"])
    put("convb", p["conv_b"])
    put("br", p["lru_br"])
    put("bi", p["lru_bi"])
    put("lam", p["lru_lambda"])
    put("pscale", p["pool_scale"])
    put("h0", h0)
    pp[:, PP_OFF["flag"]] = flag
    return pp


_NC_CACHE = {}


def kernel(**inputs):
    p = {k: np.asarray(v) for k, v in inputs.items()}
    x_prompt, x_sample = p["x_prompt"], p["x_sample"]
    state, c, c_ctx = p["state_rglru"], p["c"], p["c_ctx"]
    tabs = {"sample": _const_tables("sample"), "prompt": _const_tables("prompt")}
    shared = {k: np.ascontiguousarray(p[k], np.float32) for k in
              ("w_mod", "w_in", "w_out", "mlp_w1", "mlp_w2", "lru_wr", "lru_wi", "pool_w", "fourier_w")}
    plan = [("sample", 0), ("sample", 1), ("prompt", 0), ("idle", 0), ("prompt", 1), ("prompt", 2), ("prompt", 3),
            ("idle", 0)]
    in_maps = []
    zero_map = None
    for kind, idx in plan:
        if kind == "idle":
            if zero_map is None:
                ref_map = in_maps[0]
                zero_map = {k: np.zeros_like(v) for k, v in ref_map.items()}
            in_maps.append(zero_map)
            continue
        if kind == "sample":
            xs = x_sample[idx]
            pp = _pack_pp(c[idx], state[idx], 1.0, p)
        else:
            xs = x_prompt[4 * idx:4 * idx + 4].reshape(T, D)
            pp = _pack_pp(c_ctx, np.zeros((DEPTH, 2, DRNN), np.float32), 0.0, p)
        cl, sln, pm, cs64 = tabs[kind]
        m = {"xT": np.ascontiguousarray(xs.T, np.float32), "pp": pp, "cs64": cs64, "cl": cl, "sln": sln, "pm": pm}
        m.update(shared)
        in_maps.append(m)
    if "nc" not in _NC_CACHE:
        _NC_CACHE["nc"] = build()
    res = run_bass_kernel_spmd(_NC_CACHE["nc"], in_maps, core_ids=list(range(NCORES)))
    outs = res.results
    y_prompt = np.zeros_like(x_prompt, dtype=np.float32)
    y_sample = np.zeros_like(x_sample, dtype=np.float32)
    new_state = np.zeros((x_prompt.shape[0], DEPTH, 2, DRNN), np.float32)
    for core, (kind, idx) in enumerate(plan):
        if kind == "idle":
            continue
        yT = np.asarray(outs[core]["yT"], np.float32)
        if kind == "sample":
            y_sample[idx] = yT.T
        else:
            y_prompt[4 * idx:4 * idx + 4] = yT.T.reshape(4, 256, D)
            st = np.asarray(outs[core]["st"], np.float32).reshape(128, DEPTH, 2, 4, 4)
            new_state[4 * idx:4 * idx + 4] = st.transpose(4, 1, 2, 3, 0).reshape(4, DEPTH, 2, DRNN)
    return (y_prompt, y_sample, new_state)
```

```python
import numpy as np
import ml_dtypes
from contextlib import ExitStack
import concourse.bass as bass
import concourse.mybir as mybir
from concourse.bass_utils import run_bass_kernel_spmd
from concourse.alu_op_type import AluOpType as ALU

F32 = mybir.dt.float32
BF16 = mybir.dt.bfloat16
AF = mybir.ActivationFunctionType

D = 1024
T = 1024
DEPTH = 4
DRNN = 512
DIN = 1536
DFF = 4096
NSEG = 4
SEGP = 260
EPS = 1e-6
NCORES = 8
NSLOT = 3

PP_FIELDS = [("cond", 8), ("n1g", 32), ("n2g", 32), ("fg", 8), ("bmod", 192), ("convw", 64),
             ("convb", 16), ("br", 32), ("bi", 32), ("lam", 32), ("pscale", 8), ("h0", 32),
             ("flag", 1)]
PP_OFF = {}
_o = 0
for _n, _w in PP_FIELDS:
    PP_OFF[_n] = _o
    _o += _w
NP_ = _o


class Sched:
    ENG = ("pe", "act", "dve", "pool", "sp")

    def __init__(self):
        self.q = {e: [] for e in self.ENG}
        self.cnt = {}
        self.known = {e: {} for e in self.ENG}
        self.tiles = {}
        self.bank = 0

    def _deps(self, eng, r, w):
        need = {}
        for k in r:
            t = self.tiles.get(k)
            if t and t["w"]:
                sk, v = t["w"]
                need[sk] = max(need.get(sk, 0), v)
        for k in w:
            t = self.tiles.get(k)
            if t:
                if t["w"]:
                    sk, v = t["w"]
                    need[sk] = max(need.get(sk, 0), v)
                for sk, v in t["r"].items():
                    need[sk] = max(need.get(sk, 0), v)
        waits = []
        for sk, v in need.items():
            if sk == "pe" and eng == "pe":
                continue
            if self.known[eng].get(sk, 0) >= v:
                continue
            self.known[eng][sk] = v
            waits.append((sk, v))
        return waits

    def _record(self, ev, r, w):
        for k in w:
            self.tiles[k] = {"w": ev, "r": {}}
        for k in r:
            t = self.tiles.setdefault(k, {"w": None, "r": {}})
            t["r"][ev[0]] = max(t["r"].get(ev[0], 0), ev[1])

    def op(self, eng, fn, r=(), w=(), inc=True):
        waits = self._deps(eng, r, w)
        c = self.cnt.get(eng, 0)
        if inc:
            c += 1
            self.cnt[eng] = c
            ev = (eng, c)
        else:
            ev = (eng, c + 1)
        self.q[eng].append((waits, fn, (eng, 1) if inc else None))
        self._record(ev, r, w)

    def dma(self, queue, fn, r=(), w=(), sem="su"):
        waits = self._deps(queue, r, w)
        c = self.cnt.get(sem, 0) + 16
        self.cnt[sem] = c
        self.q[queue].append((waits, fn, (sem, 16)))
        self._record((sem, c), r, w)

    def seal(self, sem):
        tot = self.cnt.get(sem, 0)
        for t in self.tiles.values():
            if t["w"] and t["w"][0] == sem:
                t["w"] = (sem, tot)

    def next_bank(self):
        b = self.bank
        self.bank = (b + 1) % 8
        return b

    def sem_names(self):
        names = set(self.cnt.keys())
        return sorted(names)


def build(nl=DEPTH, debug=False):
    nc = bass.Bass("TRN2", target_bir_lowering=False)
    S = Sched()

    def din(name, shape):
        return nc.dram_tensor(name, list(shape), F32, kind="ExternalInput").ap()

    xT_d = din("xT", [D, T])
    pp_d = din("pp", [128, NP_])
    cs64_d = din("cs64", [256, 512])
    cl_d = nc.dram_tensor("cl", [T, T], BF16, kind="ExternalInput").ap()
    sln_d = nc.dram_tensor("sln", [T, T], BF16, kind="ExternalInput").ap()
    pm_d = nc.dram_tensor("pm", [4, T, T], BF16, kind="ExternalInput").ap()
    wmod_d = din("w_mod", [DEPTH, D, 6 * D])
    win_d = din("w_in", [DEPTH, D, DIN])
    wout_d = din("w_out", [DEPTH, D, D])
    w1_d = din("mlp_w1", [DEPTH, D, DFF])
    w2_d = din("mlp_w2", [DEPTH, DFF, D])
    wr_d = din("lru_wr", [DEPTH, 2, 8, 64, 64])
    wi_d = din("lru_wi", [DEPTH, 2, 8, 64, 64])
    pw_d = din("pool_w", [DEPTH, 4, 64, 64])
    fw_d = din("fourier_w", [DEPTH, 256, 256])
    yT_d = nc.dram_tensor("yT", [D, T], F32, kind="ExternalOutput").ap()
    st_d = nc.dram_tensor("st", [128, 128], F32, kind="ExternalOutput").ap()

    es = ExitStack()
    with es:
        def sb(name, shape, dt):
            return es.enter_context(nc.sbuf_tensor(name, list(shape), dt))

        x_sb = sb("x_sb", [128, 8, T], F32)
        hT = sb("hT", [128, 8, T], BF16)
        ring = [sb(f"ring{i}", [128, 8, 1024], BF16) for i in range(NSLOT)]
        xrp = sb("xrp", [128, 4, NSEG * SEGP], F32)
        mixB = sb("mixB", [128, 8, 1024], BF16)
        gel = sb("gel", [128, 4, T], BF16)
        xc = sb("xc", [128, T], F32)
        xcb = sb("xcb", [128, T], BF16)
        tA = sb("tA", [128, T], F32)
        tB = sb("tB", [128, T], F32)
        tC = sb("tC", [128, T], F32)
        tD = sb("tD", [128, T], F32)
        tE = sb("tE", [128, T], F32)
        tF = sb("tF", [128, T], F32)
        tG = sb("tG", [128, T], F32)
        tH = sb("tH", [128, T], F32)
        tI = sb("tI", [128, T], F32)
        tJ = sb("tJ", [128, T], F32)
        tK = sb("tK", [128, T], F32)
        tL = sb("tL", [128, T], F32)
        pp = sb("pp_sb", [128, NP_], F32)
        mods = sb("mods", [128, DEPTH * 6 * 8], F32)
        gmod = sb("gmod", [128, DEPTH * 2 * 8], F32)
        lrup = sb("lrup", [128, 4 * 32], F32)
        gw = [sb("gw0", [128, 16, 128], BF16)] * 2
        pw = [sb("pw0", [128, 2, 128], BF16)] * 2
        fw = [sb("fw0", [128, 2, 256], BF16)] * 2
        cs64 = sb("cs64_sb", [128, 2, 512], BF16)
        ones = sb("ones", [128, 128], BF16)
        s_sb = sb("s_sb", [128, 8], BF16)
        st_sb = sb("st_sb", [128, 128], F32)
        ps = [es.enter_context(nc.psum_tensor(f"ps{i}", [128, 512], F32)) for i in range(8)]

        hid0 = xrp[:].rearrange("p a b -> p (a b)").bitcast(BF16)[:, 0:8192].rearrange("p (c t) -> p c t", c=8)
        hid1 = mixB
        xp_v = mixB[:, 0:2, :].rearrange("p a (b c) -> p (a b) c", c=256)
        xfT = mixB[:, 2:4, :]
        fourT = mixB[:, 2:4, :]
        poolT = mixB[:, 4:6, :]
        Y_v = mixB[:, 4:8, :].rearrange("p a (b c) -> p (a b) c", c=512)
        xrp_v = xrp[:].rearrange("p j (s t) -> p j s t", t=SEGP)

        def ppc(name, idx=0, w=1):
            o = PP_OFF[name] + idx
            return pp[:, o:o + w]

        def modc(l, k, c):
            o = (l * 6 + k) * 8 + c
            return mods[:, o:o + 1]

        S.dma("sp", lambda e: e.dma_start(out=pp[:], in_=pp_d), w=[("pp",)], sem="su")
        S.dma("sp", lambda e: e.dma_start(out=x_sb[:], in_=xT_d.rearrange("(c p) t -> p c t", p=128)),
              w=[("x", c, h) for c in range(8) for h in range(2)], sem="su")
        S.dma("pool", lambda e: e.dma_start(out=cs64[:], in_=cs64_d.rearrange("(k p) n -> p k n", p=128)),
              w=[("cs64",)], sem="suc")
        S.seal("su")
        S.op("dve", lambda e: e.memset(ones[:], 1.0), w=[("ones",)])
        for i in range(1):
            S.op("dve", lambda e, i=i: e.memset(gw[i][:], 0.0), w=[("gw", i)])
            S.op("dve", lambda e, i=i: e.memset(pw[i][:], 0.0), w=[("pw", i)])
        S.op("dve", lambda e: e.memset(st_sb[:], 0.0), w=[("st",)])
        S.op("dve", lambda e: e.memset(xrp[:], 0.0), w=[("xrp", j) for j in range(4)])
        S.op("act", lambda e: e.activation(out=s_sb[:], in_=ppc("cond", 0, 8), func=AF.Silu),
             r=[("pp",)], w=[("s",)])
        S.op("dve", lambda e: e.tensor_scalar(lrup[:, 0:32], ppc("br", 0, 32), 0.5, None, ALU.mult),
             r=[("pp",)], w=[("lrup", 0)])
        S.op("dve", lambda e: e.tensor_scalar(lrup[:, 32:64], ppc("bi", 0, 32), 0.5, None, ALU.mult),
             r=[("pp",)], w=[("lrup", 1)])
        S.op("act", lambda e: e.activation(out=lrup[:, 96:128], in_=ppc("lam", 0, 32), func=AF.Exp, scale=-1.0),
             r=[("pp",)], w=[("lrup", 3)])
        S.op("act", lambda e: e.activation(out=lrup[:, 96:128], in_=lrup[:, 96:128], func=AF.Ln, bias=1.0),
             r=[("lrup", 3)], w=[("lrup", 3)])
        S.op("dve", lambda e: e.tensor_scalar(lrup[:, 64:96], lrup[:, 96:128], -4.0, None, ALU.mult),
             r=[("lrup", 3)], w=[("lrup", 2)])

        def hbr(l, d, j):
            o = (l * 2 + d) * 4 + j
            return lrup[:, o:o + 1]

        def hbi(l, d, j):
            o = 32 + (l * 2 + d) * 4 + j
            return lrup[:, o:o + 1]

        def kap2(l, d, j):
            o = 64 + (l * 2 + d) * 4 + j
            return lrup[:, o:o + 1]

        items = []

        def kview(ap2d):
            return ap2d.rearrange("(kc p) n -> p kc n", p=128)

        def mod_item(l, k):
            return (kview(wmod_d[l])[:, :, k * D:(k + 1) * D], 1024, ("mod", l, k))
        items.append(mod_item(0, 0))
        items.append(mod_item(0, 1))
        for l in range(nl):
            if l > 0:
                items.append(mod_item(l, 4))
                items.append(mod_item(l, 5))
            items.append((kview(win_d[l])[:, :, 0:768], 768, ("winA", l)))
            if l == 0:
                items.append(mod_item(0, 2))
            items.append((kview(win_d[l])[:, :, 768:1536], 768, ("winB", l)))
            if l == 0:
                items.append(mod_item(0, 3))
            items.append((kview(cl_d), 1024, ("cl", l)))
            items.append((kview(sln_d), 1024, ("sln", l)))
            for g in range(4):
                items.append((kview(pm_d[g]), 1024, ("pm", l, g)))
                if l == 0 and g < 2:
                    items.append(mod_item(0, 4 + g))
            if l + 1 < nl:
                items.append(mod_item(l + 1, 0))
                items.append(mod_item(l + 1, 1))
            items.append((kview(wout_d[l]), 1024, ("wout", l)))
            for tag, q in (("w1", 0), ("w1", 1), ("w2", 0), ("w1", 2), ("w2", 1), ("w1", 3), ("w2", 2), ("w2", 3)):
                if tag == "w1":
                    items.append((kview(w1_d[l])[:, :, q * 1024:(q + 1) * 1024], 1024, ("w1", l, q)))
                else:
                    items.append((kview(w2_d[l][q * 1024:(q + 1) * 1024, :]), 1024, ("w2", l, q)))
                if l + 1 < nl and tag == "w2" and q in (0, 1):
                    items.append(mod_item(l + 1, 2 + q))
        item_idx = {it[2]: i for i, it in enumerate(items)}
        loaded = [0]

        def advance(i):
            while loaded[0] < min(len(items), i + NSLOT):
                j = loaded[0]
                src, n, _ = items[j]
                slot = j % NSLOT
                if items[j][2][0] in ("cl", "sln", "pm"):
                    S.dma("sp", lambda e, slot=slot, src=src, n=n: e.dma_start(out=ring[slot][:, :, 0:n], in_=src),
                          w=[("ring", slot)], sem=f"ringS{slot}")
                else:
                    S.dma("pool", lambda e, slot=slot, src=src, n=n: e.dma_start(out=ring[slot][:, :, 0:n], in_=src),
                          w=[("ring", slot)], sem=f"ring{slot}")
                loaded[0] += 1

        def slot_of(tag):
            i = item_idx[tag]
            assert i < loaded[0], tag
            return i % NSLOT

        def mm_group(out_ap, pairs, reads, bank):
            n = len(pairs)

            def fn(e):
                ins = None
                for i, (l_, r_) in enumerate(pairs):
                    ins = e.matmul(out_ap, l_, r_, start=(i == 0), stop=(i == n - 1))
                return ins
            S.op("pe", fn, r=reads, w=[("ps", bank)])

        def small_weights(l):
            i = 0
            for d in range(2):
                for g, src in ((0, wr_d), (1, wi_d)):
                    for e_ in range(2):
                        base = (d * 2 + g) * 4
                        dst = gw[i][64 * e_:64 * e_ + 64, base:base + 4, 64 * e_:64 * e_ + 64]
                        s_ap = src[l, d].rearrange("(j e) r c -> e r j c", e=2)[e_]
                        S.dma("pool", lambda e, dst=dst, s_ap=s_ap: e.dma_start(out=dst, in_=s_ap),
                              w=[("gw", i)], sem=f"sw{i}")
            for e_ in range(2):
                dst = pw[i][64 * e_:64 * e_ + 64, 0:2, 64 * e_:64 * e_ + 64]
                s_ap = pw_d[l].rearrange("(pr e) r c -> e r pr c", e=2)[e_]
                S.dma("pool", lambda e, dst=dst, s_ap=s_ap: e.dma_start(out=dst, in_=s_ap),
                      w=[("pw", i)], sem=f"sw{i}")
            S.dma("pool", lambda e: e.dma_start(out=fw[i][:], in_=fw_d[l].rearrange("(k p) n -> p k n", p=128)),
                  w=[("fw", i)], sem=f"sw{i}")
            S.seal(f"sw{i}")

        def compute_mods(l, ks):
            for k in ks:
                advance(item_idx[("mod", l, k)])
                sl = slot_of(("mod", l, k))
                bank = S.next_bank()

                def fn(e, sl=sl, bank=bank):
                    ins = None
                    for c in range(8):
                        for kc in range(8):
                            ins = e.matmul(ps[bank][:, c:c + 1], ring[sl][:, kc, c * 128:(c + 1) * 128],
                                           s_sb[:, kc:kc + 1], start=(kc == 0), stop=(kc == 7))
                    return ins
                S.op("pe", fn, r=[("ring", sl), ("s",)], w=[("ps", bank)])
                o = (l * 6 + k) * 8
                S.op("dve", lambda e, bank=bank, o=o: e.tensor_tensor(
                    mods[:, o:o + 8], ps[bank][:, 0:8], ppc("bmod", o, 8), ALU.add),
                    r=[("ps", bank), ("pp",)], w=[("mods", l, k)])
            for which, kk, gname in ((0, 1, "n1g"), (1, 4, "n2g")):
                if kk not in ks:
                    continue
                o = (l * 6 + kk) * 8
                go = (l * 2 + which) * 8
                S.op("dve", lambda e, o=o, go=go, gname=gname, l=l: e.scalar_tensor_tensor(
                    gmod[:, go:go + 8], mods[:, o:o + 8], 1.0, ppc(gname, l * 8, 8), ALU.add, ALU.mult),
                    r=[("mods", l, kk), ("pp",)], w=[("gmod", l, which)])

        def sq_buf(which, c):
            if which == 1:
                return mixB[:, c, :], [("mixB", c)]
            return hT[:, c, :], [("hT", c, 0), ("hT", c, 1)]

        def sq_key_h(which, c, h):
            return ("mixB", c) if which == 1 else ("hT", c, h)

        def norm_square(which, c, h=None):
            ap, keys = sq_buf(which, c)
            if h is None:
                S.op("act", lambda e, c=c, ap=ap: e.activation(out=ap, in_=x_sb[:, c, :], func=AF.Square),
                     r=[("x", c, 0), ("x", c, 1)], w=keys)
            else:
                hs_ = slice(h * 512, (h + 1) * 512)
                S.op("act", lambda e, c=c, ap=ap, hs_=hs_: e.activation(out=ap[:, hs_], in_=x_sb[:, c, hs_], func=AF.Square),
                     r=[("x", c, h)], w=[sq_key_h(which, c, h)])

        def norm_half(l, which, h):
            hs_ = slice(h * 512, (h + 1) * 512)
            bank = S.next_bank()
            mm_group(ps[bank][:, :], [(ones[:, :], sq_buf(which, c)[0][:, hs_]) for c in range(8)],
                     [("ones",)] + [sq_key_h(which, c, h) for c in range(8)], bank)
            S.op("act", lambda e, bank=bank: e.activation(
                out=tA[:, hs_], in_=ps[bank][:, :], func=AF.Ln, scale=1.0 / D, bias=EPS),
                r=[("ps", bank)], w=[("tA", h)])
            S.op("act", lambda e: e.activation(out=tA[:, hs_], in_=tA[:, hs_], func=AF.Exp, scale=-0.5),
                 r=[("tA", h)], w=[("tA", h)])
            for c in range(8):
                tmp, tk = (tB, "tB") if c % 2 == 0 else (tC, "tC")
                S.op("dve", lambda e, c=c, tmp=tmp: e.tensor_tensor(
                    tmp[:, hs_], x_sb[:, c, hs_], tA[:, hs_], ALU.mult),
                    r=[("x", c, h), ("tA", h)], w=[(tk, h)])
                go = (l * 2 + which) * 8 + c
                sh = modc(l, 0 if which == 0 else 3, c)
                S.op("act", lambda e, c=c, tmp=tmp, go=go, sh=sh: e.activation(
                    out=hT[:, c, hs_], in_=tmp[:, hs_], func=AF.Identity, scale=gmod[:, go:go + 1], bias=sh),
                    r=[(tk, h), ("gmod", l, which), ("mods", l, 0 if which == 0 else 3)], w=[("hT", c, h)])

        def norm_finish(l, which, mid=None):
            for h in range(2):
                bank = S.next_bank()
                mm_group(ps[bank][:, :], [(ones[:, :], sq_buf(which, c)[0][:, h * 512:(h + 1) * 512]) for c in range(8)],
                         [("ones",)] + [sq_key_h(which, c, h) for c in range(8)], bank)
                S.op("act", lambda e, bank=bank, h=h: e.activation(
                    out=tA[:, h * 512:(h + 1) * 512], in_=ps[bank][:, :], func=AF.Ln, scale=1.0 / D, bias=EPS),
                    r=[("ps", bank)], w=[("tA", h)])
            S.op("act", lambda e: e.activation(out=tA[:, :], in_=tA[:, :], func=AF.Exp, scale=-0.5),
                 r=[("tA", 0), ("tA", 1)], w=[("tA", 0), ("tA", 1)])
            if mid is not None:
                mid()
            if which < 2:
                for h in range(2):
                    hs_ = slice(h * 512, (h + 1) * 512)
                    for c in range(8):
                        tmp, tk = (tB, "tB") if c % 2 == 0 else (tC, "tC")
                        S.op("dve", lambda e, c=c, tmp=tmp, hs_=hs_: e.tensor_tensor(
                            tmp[:, hs_], x_sb[:, c, hs_], tA[:, hs_], ALU.mult),
                            r=[("x", c, h), ("tA", h)], w=[(tk, h)])
                        go = (l * 2 + which) * 8 + c
                        sh = modc(l, 0 if which == 0 else 3, c)
                        S.op("act", lambda e, c=c, tmp=tmp, go=go, sh=sh, hs_=hs_: e.activation(
                            out=hT[:, c, hs_], in_=tmp[:, hs_], func=AF.Identity, scale=gmod[:, go:go + 1], bias=sh),
                            r=[(tk, h), ("gmod", l, which), ("mods", l, 0 if which == 0 else 3)], w=[("hT", c, h)])
            else:
                ftmps = ((tB, "tB"), (tC, "tC"), (tD, "tD"), (tE, "tE"), (tF, "tF"), (tG, "tG"), (tH, "tH"), (tI, "tI"))
                for c in range(8):
                    tmp, tk = ftmps[c]
                    S.op("dve", lambda e, c=c, tmp=tmp: e.tensor_tensor(tmp[:, :], x_sb[:, c, :], tA[:, :], ALU.mult),
                         r=[("x", c, 0), ("x", c, 1), ("tA", 0), ("tA", 1)], w=[(tk, 0), (tk, 1)])
                    S.op("act", lambda e, c=c, tmp=tmp: e.activation(
                        out=tmp[:, :], in_=tmp[:, :], func=AF.Copy, scale=ppc("fg", c)),
                        r=[(tk, 0), (tk, 1), ("pp",)], w=[(tk, 0), (tk, 1)])
                    S.dma("sp", lambda e, c=c, tmp=tmp: e.dma_start(out=yT_d[c * 128:(c + 1) * 128, :], in_=tmp[:, :]),
                          r=[(tk, 0), (tk, 1)], sem="out" + tk)

        def evac_copy(eng, out_ap, bank, wkeys, in_ap=None):
            src = ps[bank][:, :] if in_ap is None else in_ap
            if eng == "act":
                S.op("act", lambda e: e.activation(out=out_ap, in_=src, func=AF.Copy), r=[("ps", bank)], w=wkeys)
            else:
                S.op("dve", lambda e: e.tensor_copy(out_ap, src), r=[("ps", bank)], w=wkeys)

        def hT_half(h):
            return [("hT", c, h) for c in range(8)]

        def lru_pre(l, j):
            S.op("dve", lambda e: e.tensor_scalar(xrp_v[:, j, 1:4, 0:2], xrp_v[:, j, 0:3, 256:258],
                                                   ppc("flag"), None, ALU.mult),
                 r=[("xrp", j), ("pp",)], w=[("xrp", j)])
            S.op("dve", lambda e: e.tensor_scalar(xrp_v[:, j, 0:3, 258:259], xrp_v[:, j, 1:4, 2:3],
                                                   ppc("flag"), None, ALU.mult),
                 r=[("xrp", j), ("pp",)], w=[("xrp", j)])
            xc_v = xc[:, :].rearrange("p (s t) -> p s t", t=256)

            def cw(k):
                o = (l * 4 + k) * 4 + j
                return ppc("convw", o)
            S.op("dve", lambda e: e.tensor_scalar(xc_v, xrp_v[:, j, :, 0:256], cw(0), ppc("convb", l * 4 + j),
                                                   ALU.mult, ALU.add),
                 r=[("xrp", j), ("pp",)], w=[("xc",)])
            for k in range(1, 4):
                S.op("dve", lambda e, k=k: e.scalar_tensor_tensor(xc_v, xrp_v[:, j, :, k:k + 256], cw(k), xc_v,
                                                                  ALU.mult, ALU.add),
                     r=[("xrp", j), ("xc",), ("pp",)], w=[("xc",)])
            S.op("act", lambda e: e.activation(out=xcb[:, :], in_=xc[:, :], func=AF.Copy), r=[("xc",)], w=[("xcb",)])

        TSS = {
            (0, 0): (tA, tB, tC, tD, ("tA", "tB", "tC", "tD")), (0, 1): (tI, tJ, tC, tD, ("tI", "tJ", "tC", "tD")),
            (1, 0): (tE, tF, tG, tH, ("tE", "tF", "tG", "tH")), (1, 1): (tK, tL, tG, tH, ("tK", "tL", "tG", "tH")),
        }

        def k2(name):
            return [(name, 0), (name, 1)]

        def lru_chunk(l, j, defer_post=False):
            i = 0
            TS = (TSS[(0, j % 2)], TSS[(1, j % 2)])
            for d in range(2):
                A_, B_, C_, D_, nm = TS[d]
                for g, dst, dk, hb in ((0, A_, nm[0], hbr(l, d, j)), (1, B_, nm[1], hbi(l, d, j))):
                    for h in range(2):
                        bank = S.next_bank()
                        mm_group(ps[bank][:, :], [(gw[i][:, (d * 2 + g) * 4 + j, :], xcb[:, h * 512:(h + 1) * 512])],
                                 [("gw", i), ("xcb",)], bank)
                        S.op("act", lambda e, bank=bank, dst=dst, h=h, hb=hb: e.activation(
                            out=dst[:, h * 512:(h + 1) * 512], in_=ps[bank][:, :], func=AF.Tanh, scale=0.5, bias=hb),
                            r=[("ps", bank), ("lrup", g)], w=[(dk, h)])
            for d in range(2):
                A_, B_, C_, D_, nm = TS[d]
                kk = kap2(l, d, j)
                S.op("act", lambda e, A_=A_, kk=kk: e.activation(out=A_[:, :], in_=A_[:, :], func=AF.Exp, scale=kk, bias=kk),
                     r=k2(nm[0]) + [("lrup", 2)], w=k2(nm[0]))
            for d in range(2):
                A_, B_, C_, D_, nm = TS[d]
                S.op("act", lambda e, A_=A_, C_=C_: e.activation(out=C_[:, :], in_=A_[:, :], func=AF.Square),
                     r=k2(nm[0]), w=k2(nm[2]))
            for d in range(2):
                A_, B_, C_, D_, nm = TS[d]
                S.op("act", lambda e, C_=C_: e.activation(out=C_[:, :], in_=C_[:, :], func=AF.Sqrt, scale=-1.0, bias=1.0),
                     r=k2(nm[2]), w=k2(nm[2]))
            for d in range(2):
                A_, B_, C_, D_, nm = TS[d]
                S.op("dve", lambda e, B_=B_: e.scalar_tensor_tensor(B_[:, :], B_[:, :], 1.0, xc[:, :], ALU.add, ALU.mult),
                     r=k2(nm[1]) + [("xc",)], w=k2(nm[1]))
            if j + 1 < 4:
                lru_pre(l, j + 1)
            for d in range(2):
                A_, B_, C_, D_, nm = TS[d]
                S.op("dve", lambda e, B_=B_, C_=C_: e.scalar_tensor_tensor(B_[:, :], C_[:, :], 0.5, B_[:, :], ALU.mult, ALU.mult),
                     r=k2(nm[1]) + k2(nm[2]), w=k2(nm[1]))
                if d == 0:
                    bnd = A_[:, 256:1024].rearrange("p (s t) -> p s t", t=256)[:, :, 0:1]
                else:
                    bnd = A_[:, 0:768].rearrange("p (s t) -> p s t", t=256)[:, :, 255:256]
                S.op("dve", lambda e, bnd=bnd: e.tensor_scalar(bnd, bnd, ppc("flag"), None, ALU.mult),
                     r=k2(nm[0]) + [("pp",)], w=k2(nm[0]))
                h0 = ppc("h0", (l * 2 + d) * 4 + j)
                so = ((l * 2 + d) * 4 + j) * 4
                if d == 0:
                    S.op("dve", lambda e, A_=A_, B_=B_, D_=D_, h0=h0: e.tensor_tensor_scan(
                        D_[:, :], A_[:, :], B_[:, :], h0, ALU.mult, ALU.add),
                        r=k2(nm[0]) + k2(nm[1]) + [("pp",)], w=k2(nm[3]))
                    fin = D_[:, :].rearrange("p (s t) -> p s t", t=256)[:, :, 255:256]
                else:
                    def rv(t_):
                        return bass.AP(t_, T - 1, [[T, 128], [-1, T]])
                    S.op("dve", lambda e, A_=A_, B_=B_, D_=D_, h0=h0, rv=rv: e.tensor_tensor_scan(
                        rv(D_), rv(A_), rv(B_), h0, ALU.mult, ALU.add),
                        r=k2(nm[0]) + k2(nm[1]) + [("pp",)], w=k2(nm[3]))
                    fin = D_[:, :].rearrange("p (s t) -> p s t", t=256)[:, :, 0:1]
                S.op("dve", lambda e, so=so, fin=fin: e.tensor_copy(
                    st_sb[:, so:so + 4].rearrange("p (s o) -> p s o", o=1), fin),
                    r=k2(nm[3]), w=[("st",)])
            def post():
                S.op("dve", lambda e: e.tensor_tensor(tD[:, :], tD[:, :], tH[:, :], ALU.add),
                     r=k2("tD") + k2("tH"), w=k2("tD"))
                S.op("dve", lambda e: e.tensor_tensor(hT[:, j, :], tD[:, :], gel[:, j, :], ALU.mult),
                     r=k2("tD") + [("gel", j)], w=[("hT", j, 0), ("hT", j, 1)])
            if defer_post:
                return post
            post()

        def layer(l):
            i = 0
            if l == 0:
                small_weights(0)
                for c in range(8):
                    norm_square(0, c)
            if l == 0:
                norm_finish(l, 0)
            else:
                compute_mods(l, [4, 5])
            advance(item_idx[("winA", l)])
            sA = slot_of(("winA", l))
            for h in range(2):
                for m in range(6):
                    bank = S.next_bank()
                    mm_group(ps[bank][:, :],
                             [(ring[sA][:, kc, m * 128:(m + 1) * 128], hT[:, kc, h * 512:(h + 1) * 512]) for kc in range(8)],
                             [("ring", sA)] + hT_half(h), bank)
                    if m < 4:
                        S.op("act", lambda e, bank=bank, m=m, h=h: e.activation(
                            out=xrp_v[:, m, 2 * h:2 * h + 2, 2:258],
                            in_=ps[bank][:, :].rearrange("p (s t) -> p s t", t=256), func=AF.Copy),
                            r=[("ps", bank)], w=[("xrp", m)])
                    else:
                        S.op("act", lambda e, bank=bank, m=m, h=h: e.activation(
                            out=gel[:, m - 4, h * 512:(h + 1) * 512], in_=ps[bank][:, :], func=AF.Gelu),
                            r=[("ps", bank)], w=[("gel", m - 4)])
                    if m == 0 and h == 1:
                        lru_pre(l, 0)
                    if m == 3 and h == 1:
                        wo_post0 = lru_chunk(l, 0, defer_post=True)
            if l == 0:
                compute_mods(0, [2])
            advance(item_idx[("winB", l)])
            sB = slot_of(("winB", l))
            def winB_cols(ms):
                for m in ms:
                    for h in range(2):
                        bank = S.next_bank()
                        mm_group(ps[bank][:, :],
                                 [(ring[sB][:, kc, m * 128:(m + 1) * 128], hT[:, kc, h * 512:(h + 1) * 512]) for kc in range(8)],
                                 [("ring", sB)] + hT_half(h), bank)
                        if m < 2:
                            S.op("act", lambda e, bank=bank, m=m, h=h: e.activation(
                                out=gel[:, m + 2, h * 512:(h + 1) * 512], in_=ps[bank][:, :], func=AF.Gelu),
                                r=[("ps", bank)], w=[("gel", m + 2)])
                        else:
                            evac_copy("act", xfT[:, m - 4, h * 512:(h + 1) * 512], bank, [("mixB", 2 + m - 4)])
            winB_cols((0, 1))
            post0 = wo_post0
            winB_cols((4, 5))
            for tc in range(8):
                bank = S.next_bank()
                mm_group(ps[bank][:, 0:256],
                         [(hT[:, kc, tc * 128:(tc + 1) * 128], ring[sB][:, kc, 256:512]) for kc in range(8)],
                         [("ring", sB)] + hT_half(tc // 4), bank)
                evac_copy("act", xp_v[:, tc, :], bank, [("mixB", tc // 4)], in_ap=ps[bank][:, 0:256])
            post0()
            if l == 0:
                compute_mods(0, [3])
            advance(item_idx[("cl", l)])
            sC, sS = slot_of(("cl", l)), slot_of(("sln", l))
            for tc in range(8):
                bank = S.next_bank()
                mm_group(ps[bank][:, :], [(xfT[:, kc, tc * 128:(tc + 1) * 128], cs64[:, kc, :]) for kc in range(2)],
                         [("mixB", 2), ("mixB", 3), ("cs64",)], bank)
                evac_copy("act", Y_v[:, tc, :], bank, [("mixB", 4 + tc // 2)])

            def dft(jj):
                for h in range(2):
                    bank = S.next_bank()
                    pairs = []
                    for tc in range(8):
                        pairs.append((Y_v[:, tc, jj * 128:(jj + 1) * 128], ring[sC][:, tc, h * 512:(h + 1) * 512]))
                        pairs.append((Y_v[:, tc, 256 + jj * 128:256 + (jj + 1) * 128], ring[sS][:, tc, h * 512:(h + 1) * 512]))
                    mm_group(ps[bank][:, :], pairs, [("ring", sC), ("ring", sS)] + [("mixB", 4 + q) for q in range(4)], bank)
                    evac_copy("act", fourT[:, jj, h * 512:(h + 1) * 512], bank, [("mixB", 2 + jj)])
            lru_chunk(l, 1)
            dft(0)
            dft(1)
            for j2 in range(2):
                for h in range(2):
                    bank = S.next_bank()
                    mm_group(ps[bank][:, :],
                             [(fw[i][:, kc, j2 * 128:(j2 + 1) * 128], fourT[:, kc, h * 512:(h + 1) * 512]) for kc in range(2)],
                             [("fw", i), ("mixB", 2), ("mixB", 3)], bank)
                    evac_copy("act", hT[:, 6 + j2, h * 512:(h + 1) * 512], bank, [("hT", 6 + j2, h)])

            def pool_group(g):
                advance(item_idx[("pm", l, g)])
                sP = slot_of(("pm", l, g))
                pr, gi = g // 2, g % 2
                for h in range(2):
                    bank = S.next_bank()
                    mm_group(ps[bank][:, :],
                             [(xp_v[:, tc, pr * 128:(pr + 1) * 128], ring[sP][:, tc, h * 512:(h + 1) * 512]) for tc in range(8)],
                             [("ring", sP), ("mixB", 0), ("mixB", 1)], bank)
                    lo = gi * 64
                    S.op("act", lambda e, bank=bank, lo=lo, pr=pr, h=h: e.activation(
                        out=poolT[lo:lo + 64, pr, h * 512:(h + 1) * 512], in_=ps[bank][lo:lo + 64, :], func=AF.Copy),
                        r=[("ps", bank)], w=[("mixB", 4 + pr)])
            lru_chunk(l, 2)
            pool_group(0)
            if l == 0:
                compute_mods(0, [4])
            pool_group(1)
            if l == 0:
                compute_mods(0, [5])
            pool_group(2)
            pool_group(3)
            for pr in range(2):
                for h in range(2):
                    bank = S.next_bank()
                    mm_group(ps[bank][:, :], [(pw[i][:, pr, :], poolT[:, pr, h * 512:(h + 1) * 512])],
                             [("pw", i), ("mixB", 4 + pr)], bank)
                    S.op("act", lambda e, bank=bank, pr=pr, h=h: e.activation(
                        out=hT[:, 4 + pr, h * 512:(h + 1) * 512], in_=ps[bank][:, :], func=AF.Copy,
                        scale=ppc("pscale", l * 2 + pr)),
                        r=[("ps", bank), ("pp",)], w=[("hT", 4 + pr, h)])
            lru_chunk(l, 3)
            if l + 1 < nl:
                compute_mods(l + 1, [0, 1])
            advance(item_idx[("wout", l)])
            sO = slot_of(("wout", l))
            for h in range(2):
                for m in range(8):
                    bank = S.next_bank()
                    mm_group(ps[bank][:, :],
                             [(ring[sO][:, kc, m * 128:(m + 1) * 128], hT[:, kc, h * 512:(h + 1) * 512]) for kc in range(8)],
                             [("ring", sO)] + hT_half(h), bank)
                    xs = x_sb[:, m, h * 512:(h + 1) * 512]
                    S.op("dve", lambda e, bank=bank, xs=xs, m=m: e.scalar_tensor_tensor(
                        xs, ps[bank][:, :], modc(l, 2, m), xs, ALU.mult, ALU.add),
                        r=[("ps", bank), ("x", m, h), ("mods", l, 2)], w=[("x", m, h)])
                    norm_square(1, m, h)
                norm_half(l, 1, h)
            hid = (hid0, hid1)
            hkey = ("xrp", "mixB")

            def w1_phase(q):
                advance(item_idx[("w1", l, q)])
                s1 = slot_of(("w1", l, q))
                hb_, hk = hid[q % 2], hkey[q % 2]
                for h in range(2):
                    for mm_ in range(8):
                        bank = S.next_bank()
                        mm_group(ps[bank][:, :],
                                 [(ring[s1][:, kc, mm_ * 128:(mm_ + 1) * 128], hT[:, kc, h * 512:(h + 1) * 512]) for kc in range(8)],
                                 [("ring", s1)] + hT_half(h), bank)
                        tmp, tk = ((tC, "tC") if (mm_ % 2 == 0) else (tD, "tD"))
                        S.op("act", lambda e, bank=bank, tmp=tmp, h=h: e.activation(
                            out=tmp[:, h * 512:(h + 1) * 512], in_=ps[bank][:, :], func=AF.Relu),
                            r=[("ps", bank)], w=[(tk, h)])
                        wk = [("xrp", j) for j in range(4)] if hk == "xrp" else [("mixB", mm_)]
                        S.op("dve", lambda e, bank=bank, tmp=tmp, h=h, hb_=hb_, mm_=mm_: e.tensor_tensor(
                            hb_[:, mm_, h * 512:(h + 1) * 512], ps[bank][:, :], tmp[:, h * 512:(h + 1) * 512], ALU.mult),
                            r=[("ps", bank), (tk, h)], w=wk)

            def w2_phase(q, last=False):
                advance(item_idx[("w2", l, q)])
                s2 = slot_of(("w2", l, q))
                hb_, hk = hid[q % 2], hkey[q % 2]
                rk = [("xrp", j) for j in range(4)] if hk == "xrp" else [("mixB", c) for c in range(8)]
                order = [(m, h) for h in range(2) for m in range(8)] if last else [(m, h) for m in range(8) for h in range(2)]
                for (m, h) in order:
                    bank = S.next_bank()
                    mm_group(ps[bank][:, :],
                             [(ring[s2][:, kc, m * 128:(m + 1) * 128], hb_[:, kc, h * 512:(h + 1) * 512]) for kc in range(8)],
                             [("ring", s2)] + rk, bank)
                    xs = x_sb[:, m, h * 512:(h + 1) * 512]
                    S.op("dve", lambda e, bank=bank, xs=xs, m=m: e.scalar_tensor_tensor(
                        xs, ps[bank][:, :], modc(l, 5, m), xs, ALU.mult, ALU.add),
                        r=[("ps", bank), ("x", m, h), ("mods", l, 5)], w=[("x", m, h)])
                    if last:
                        if l + 1 < nl:
                            norm_square(0, m, h)
                            if m == 7:
                                norm_half(l + 1, 0, h)
                        elif h == 1:
                            norm_square(2, m)
            w1_phase(0)
            if l + 1 < nl:
                small_weights(l + 1)
            w1_phase(1)
            w2_phase(0)
            if l + 1 < nl:
                compute_mods(l + 1, [2])
            w1_phase(2)
            w2_phase(1)
            if l + 1 < nl:
                compute_mods(l + 1, [3])
            w1_phase(3)
            w2_phase(2)
            w2_phase(3, last=True)
            if l + 1 < nl:
                S.op("dve", lambda e: e.memset(xrp[:], 0.0), r=[], w=[("xrp", j) for j in range(4)])

        advance(0)
        compute_mods(0, [0, 1])
        for l in range(nl):
            layer(l)
        norm_finish(nl - 1, 2)
        S.dma("sp", lambda e: e.dma_start(out=st_d, in_=st_sb[:]), r=[("st",)], sem="outS")
        out_sems = [n for n in S.cnt if n.startswith("out")]

        sems = {n: es.enter_context(nc.semaphore(f"s_{n}")) for n in S.sem_names()}
        block = es.enter_context(nc.Block())

        def emit(eng_name):
            def body(e):
                for waits, fn, inc in S.q[eng_name]:
                    for sk, v in waits:
                        e.wait_ge(sems[sk], v)
                    ins = fn(e)
                    if inc is not None:
                        ins.then_inc(sems[inc[0]], inc[1])
                if eng_name == "sp":
                    for n in out_sems:
                        e.wait_ge(sems[n], S.cnt[n])
            return body
        block.tensor(emit("pe"))
        block.scalar(emit("act"))
        block.vector(emit("dve"))
        block.gpsimd(emit("pool"))
        block.sync(emit("sp"))
    return nc


def _bounds(n, w):
    idx = np.arange(n)
    lo = np.clip(idx - w // 2, 0, n)
    hi = np.clip(idx - w // 2 + w, 0, n)
    return lo, hi


def _pool1d_mat(n, w):
    lo, hi = _bounds(n, w)
    P = np.zeros((n, n), np.float64)
    for t in range(n):
        P[t, lo[t]:hi[t]] = 1.0 / (hi[t] - lo[t])
    return P


def _const_tables(kind):
    windows = (2, 4, 8, 16)
    if kind == "sample":
        L = 1024
        t = np.arange(L)
        ang = 2.0 * np.pi * ((t[:, None] * t[None, :]) % L) / L
        sc = 1.0 / np.sqrt(L * 64.0)
        cl = np.cos(ang) * sc
        sln = -np.sin(ang) * sc
        pm = []
        for w in windows:
            P = np.kron(_pool1d_mat(16, w), _pool1d_mat(64, w))
            pm.append((P - np.eye(L)).T)
    else:
        L = 256
        t = np.arange(L)
        ang = 2.0 * np.pi * ((t[:, None] * t[None, :]) % L) / L
        sc = 1.0 / np.sqrt(L * 64.0)
        eye4 = np.eye(4)
        cl = np.kron(eye4, np.cos(ang) * sc)
        sln = np.kron(eye4, -np.sin(ang) * sc)
        pm = []
        for w in windows:
            P = np.kron(eye4, _pool1d_mat(L, w))
            pm.append((P - np.eye(4 * L)).T)
    k = np.arange(64)
    a64 = 2.0 * np.pi * ((k[:, None] * k[None, :]) % 64) / 64.0
    c64 = np.kron(np.eye(4), np.cos(a64))
    s64 = np.kron(np.eye(4), np.sin(a64))
    cs64 = np.concatenate([c64, s64], axis=1)
    bf = ml_dtypes.bfloat16
    return (np.ascontiguousarray(cl.astype(np.float32).astype(bf)), np.ascontiguousarray(sln.astype(np.float32).astype(bf)),
            np.ascontiguousarray(np.stack(pm).astype(np.float32).astype(bf)), np.ascontiguousarray(cs64, np.float32))


def _pack_pp(cond, h0, flag, p):
    pp = np.zeros((128, NP_), np.float32)

    def put(name, arr):
        a = np.asarray(arr, np.float32)
        lead = a.shape[:-1]
        c = a.shape[-1] // 128
        a = a.reshape(lead + (c, 128))
        a = np.moveaxis(a, -1, 0).reshape(128, -1)
        o = PP_OFF[name]
        pp[:, o:o + a.shape[1]] = a
    put("cond", cond)
    put("n1g", p["norm1_g"])
    put("n2g", p["norm2_g"])
    put("fg", p["final_g"])
    put("bmod", np.asarray(p["b_mod"]).reshape(DEPTH, 6, D))
    put("convw", p["conv_w"])
    put("convb", p["conv_b"])
    put("br", p["lru_br"])
    put("bi", p["lru_bi"])
    put("lam", p["lru_lambda"])
    put("pscale", p["pool_scale"])
    put("h0", h0)
    pp[:, PP_OFF["flag"]] = flag
    return pp


_NC_CACHE = {}


def kernel(**inputs):
    p = {k: np.asarray(v) for k, v in inputs.items()}
    x_prompt, x_sample = p["x_prompt"], p["x_sample"]
    state, c, c_ctx = p["state_rglru"], p["c"], p["c_ctx"]
    tabs = {"sample": _const_tables("sample"), "prompt": _const_tables("prompt")}
    shared = {k: np.ascontiguousarray(p[k], np.float32) for k in
              ("w_mod", "w_in", "w_out", "mlp_w1", "mlp_w2", "lru_wr", "lru_wi", "pool_w", "fourier_w")}
    plan = [("sample", 0), ("sample", 1), ("prompt", 0), ("idle", 0), ("prompt", 1), ("prompt", 2), ("prompt", 3),
            ("idle", 0)]
    in_maps = []
    zero_map = None
    for kind, idx in plan:
        if kind == "idle":
            if zero_map is None:
                ref_map = in_maps[0]
                zero_map = {k: np.zeros_like(v) for k, v in ref_map.items()}
            in_maps.append(zero_map)
            continue
        if kind == "sample":
            xs = x_sample[idx]
            pp = _pack_pp(c[idx], state[idx], 1.0, p)
        else:
            xs = x_prompt[4 * idx:4 * idx + 4].reshape(T, D)
            pp = _pack_pp(c_ctx, np.zeros((DEPTH, 2, DRNN), np.float32), 0.0, p)
        cl, sln, pm, cs64 = tabs[kind]
        m = {"xT": np.ascontiguousarray(xs.T, np.float32), "pp": pp, "cs64": cs64, "cl": cl, "sln": sln, "pm": pm}
        m.update(shared)
        in_maps.append(m)
    if "nc" not in _NC_CACHE:
        _NC_CACHE["nc"] = build()
    res = run_bass_kernel_spmd(_NC_CACHE["nc"], in_maps, core_ids=list(range(NCORES)))
    outs = res.results
    y_prompt = np.zeros_like(x_prompt, dtype=np.float32)
    y_sample = np.zeros_like(x_sample, dtype=np.float32)
    new_state = np.zeros((x_prompt.shape[0], DEPTH, 2, DRNN), np.float32)
    for core, (kind, idx) in enumerate(plan):
        if kind == "idle":
            continue
        yT = np.asarray(outs[core]["yT"], np.float32)
        if kind == "sample":
            y_sample[idx] = yT.T
        else:
            y_prompt[4 * idx:4 * idx + 4] = yT.T.reshape(4, 256, D)
            st = np.asarray(outs[core]["st"], np.float32).reshape(128, DEPTH, 2, 4, 4)
            new_state[4 * idx:4 * idx + 4] = st.transpose(4, 1, 2, 3, 0).reshape(4, DEPTH, 2, DRNN)
    return (y_prompt, y_sample, new_state)
```
